# Optimizing a Trainium2 kernel written in Bass

```python
import functools
import jax, jax.numpy as jnp
from jax import lax
import numpy as np

D_MODEL = 4096
BATCH = 4
SEQ = 2048
DEPTH = 2
DEC_BATCH = 8
DEC_SEQ = 4
PAST_LEN = 16384
PAGE_SIZE = 128

HG_HEADS = 8
HG_DK = 128
HG_DV = 128
HG_W = HG_HEADS * HG_DV
GLA_CHUNK = 64
EXP_CLIP = 60.0
RW_HEADS = 16
RW_HD = 64
RW_W = RW_HEADS * RW_HD
RW_DECAY_LORA = 64
RW_AAA_LORA = 64
RW_GATE_LORA = 160
RW_GN_EPS = 64e-5
RW_COLS = 3 * RW_W + RW_DECAY_LORA + RW_AAA_LORA + RW_GATE_LORA
AT_HEADS = 16
AT_KV_HEADS = 4
HEAD_DIM = 128
AT_W = AT_HEADS * HEAD_DIM
KV_W = AT_KV_HEADS * HEAD_DIM
IDX_HEADS = 8
IDX_DIM = 128
TOPK_MAX = 256
Q_BLOCK = 128
NEG_BIG = -1e30
HG_COLS = 2 * HG_HEADS * HG_DK + 2 * HG_W
AT_COLS = AT_W + 2 * KV_W + IDX_HEADS * IDX_DIM + IDX_DIM + IDX_HEADS
IN_COLS = HG_COLS + RW_COLS + AT_COLS
D_FF = -(-8 * D_MODEL // (3 * 256)) * 256
NORM_EPS = 1e-6

kernel_name = 'hybrid_hgrn2_rwkv7_dsa_decoder_step'


def _split(a, sizes):
    offs = np.cumsum(sizes)[:-1].tolist()
    return jnp.split(a, offs, axis=-1)


def rms_norm(x, g, eps=NORM_EPS):
    xf = x.astype(jnp.float32)
    y = xf * lax.rsqrt(jnp.mean(xf * xf, axis=-1, keepdims=True) + eps)
    return (y * g.astype(jnp.float32)).astype(x.dtype)


def chunked_gla(q, k, v, log_f, S0):
    B, T, H, DK = q.shape
    DV = v.shape[-1]
    C = GLA_CHUNK if T % GLA_CHUNK == 0 else T
    n = T // C
    to_chunks = lambda a: a.reshape(B, n, C, H, a.shape[-1]).swapaxes(0, 1)
    tril = jnp.tril(jnp.ones((C, C), dtype=bool))[None, :, :, None, None]

    def step(S, inp):
        qc, kc, vc, gc = inp
        b = jnp.cumsum(gc, axis=1)
        o_inter = jnp.einsum('bthk,bhkv->bthv', qc * jnp.exp(b), S)
        diff = b[:, :, None] - b[:, None, :]
        decay = jnp.where(tril, jnp.exp(jnp.where(tril, diff, 0.0)), 0.0)
        A = jnp.einsum('bthk,bshk,btshk->bths', qc, kc, decay)
        o_intra = jnp.einsum('bths,bshv->bthv', A, vc)
        b_last = b[:, -1]
        kd = kc * jnp.exp(b_last[:, None] - b)
        S_new = jnp.exp(b_last)[..., None] * S + jnp.einsum('bshk,bshv->bhkv', kd, vc)
        return S_new, o_inter + o_intra

    S, o = lax.scan(step, S0.astype(jnp.float32),
                    (to_chunks(q), to_chunks(k), to_chunks(v), to_chunks(log_f)))
    return o.swapaxes(0, 1).reshape(B, T, H, DV), S


def hgrn2_mixer(pq, pf, pi, pg, lb, norm_g, S0):
    f32 = jnp.float32
    B, T, _ = pq.shape
    q = jax.nn.silu(pq.astype(f32)).reshape(B, T, HG_HEADS, HG_DK) * (HG_DK ** -0.5)
    fr = pf.astype(f32).reshape(B, T, HG_HEADS, HG_DK)
    lbh = lb.astype(f32).reshape(HG_HEADS, HG_DK)
    log_f = jax.nn.log_sigmoid(fr) + jnp.log1p(lbh * jnp.exp(jnp.minimum(-fr, EXP_CLIP)))
    k = (1.0 - lbh) * jax.nn.sigmoid(-fr)
    v = pi.astype(f32).reshape(B, T, HG_HEADS, HG_DV)
    o, S = chunked_gla(q, k, v, log_f, S0)
    o = rms_norm(o, norm_g.reshape(HG_HEADS, HG_DV)) * jax.nn.sigmoid(pg.astype(f32)).reshape(B, T, HG_HEADS, HG_DV)
    return o.reshape(B, T, HG_W), S


def rwkv7_mixer(p, shift_prev, mu, w0, w2, a0, a2, g2, k_k, k_a, r_k, lnx_w, lnx_b, S0):
    f32 = jnp.float32
    B, T, _ = p.shape
    pf = p.astype(f32)
    prev = jnp.concatenate([shift_prev.astype(f32)[:, None], pf[:, :-1]], axis=1)
    xs = pf + (prev - pf) * mu.astype(f32)
    r, k, v, xw, xa, xg = _split(xs, [RW_W, RW_W, RW_W, RW_DECAY_LORA, RW_AAA_LORA, RW_GATE_LORA])
    w = -jax.nn.softplus(-(w0.astype(f32) + jnp.tanh(xw) @ w2.astype(f32))) - 0.5
    decay = jnp.exp(-jnp.exp(w))
    a = jax.nn.sigmoid(a0.astype(f32) + xa @ a2.astype(f32))
    g = jax.nn.sigmoid(xg) @ g2.astype(f32)
    heads = lambda t: t.reshape(B, T, RW_HEADS, RW_HD)
    kk = heads(k * k_k.astype(f32))
    kk = kk * lax.rsqrt(jnp.maximum(jnp.sum(kk * kk, axis=-1, keepdims=True), 1e-24))
    k = k * (1.0 + (a - 1.0) * k_a.astype(f32))
    r_h, k_h, v_h, w_h, a_h = heads(r), heads(k), heads(v), heads(decay), heads(a)

    def step(S, inp):
        rt, wt, kt, vt, kkt, at = inp
        sa = jnp.einsum('bhvk,bhk->bhv', S, -kkt)
        S = S * wt[:, :, None, :] + sa[..., None] * (kkt * at)[:, :, None, :] + vt[..., None] * kt[:, :, None, :]
        return S, jnp.einsum('bhvk,bhk->bhv', S, rt)

    tm = lambda t: jnp.swapaxes(t, 0, 1)
    S, y = lax.scan(step, S0.astype(f32), (tm(r_h), tm(w_h), tm(k_h), tm(v_h), tm(kk), tm(a_h)))
    y = tm(y)
    mean = jnp.mean(y, axis=-1, keepdims=True)
    var = jnp.mean(jnp.square(y - mean), axis=-1, keepdims=True)
    y = ((y - mean) * lax.rsqrt(var + RW_GN_EPS)).reshape(B, T, RW_W) * lnx_w.astype(f32) + lnx_b.astype(f32)
    bonus = jnp.sum(r_h * k_h * r_k.astype(f32), axis=-1, keepdims=True) * v_h
    out = (y + bonus.reshape(B, T, RW_W)) * g
    return out, S, pf[:, -1]


def indexer_topk(qi, wi, ki, q_pos, n_sel):
    s = jnp.einsum('bthd,bsd->bths', qi.astype(jnp.float32), ki.astype(jnp.float32))
    score = jnp.einsum('bths,bth->bts', jax.nn.relu(s), wi.astype(jnp.float32))
    causal = jnp.arange(ki.shape[1])[None, :] <= q_pos[:, None]
    score = jnp.where(causal[None], score, NEG_BIG)
    _, idx = lax.top_k(score, n_sel)
    valid = idx <= q_pos[None, :, None]
    return idx, valid


def attend_selected(q, k_sel, v_sel, valid):
    B, T = q.shape[:2]
    qg = q.astype(jnp.float32).reshape(B, T, AT_KV_HEADS, AT_HEADS // AT_KV_HEADS, HEAD_DIM)
    s = jnp.einsum('btngd,btknd->btngk', qg, k_sel.astype(jnp.float32)) * (HEAD_DIM ** -0.5)
    s = jnp.where(valid[:, :, None, None, :], s, NEG_BIG)
    p = jax.nn.softmax(s, axis=-1)
    o = jnp.einsum('btngk,btknd->btngd', p, v_sel.astype(jnp.float32))
    return o.reshape(B, T, AT_W)


_gather_rows = jax.vmap(lambda src, ii: src[ii])


def dsa_prompt(q, k, v, qi, wi, ki):
    B, T = q.shape[:2]
    n_sel = min(TOPK_MAX, T // 4)
    qb = min(Q_BLOCK, T)
    nb = T // qb
    blocks = lambda a: a.reshape(B, nb, qb, *a.shape[2:]).swapaxes(0, 1)
    pos = jnp.arange(T, dtype=jnp.int32).reshape(nb, qb)

    def one(args):
        q_b, qi_b, wi_b, p_b = args
        idx, valid = indexer_topk(qi_b, wi_b, ki, p_b, n_sel)
        return attend_selected(q_b, _gather_rows(k, idx), _gather_rows(v, idx), valid)

    o = lax.map(one, (blocks(q), blocks(qi), blocks(wi), pos))
    return o.swapaxes(0, 1).reshape(B, T, AT_W)


def dsa_sample(q, k, v, qi, wi, ki, cache_k_l, cache_v_l, cache_kidx_l, page_table):
    DB, DS = q.shape[:2]
    past = page_table.shape[1] * PAGE_SIZE
    n_sel = min(TOPK_MAX, (past + DS) // 4)
    ki_past = cache_kidx_l[page_table].reshape(DB, past, IDX_DIM)
    ki_all = jnp.concatenate([ki_past.astype(jnp.float32), ki.astype(jnp.float32)], axis=1)
    q_pos = past + jnp.arange(DS, dtype=jnp.int32)
    idx, valid = indexer_topk(qi, wi, ki_all, q_pos, n_sel)
    in_past = (idx < past)[..., None, None]
    ip = jnp.minimum(idx, past - 1)
    phys = page_table[jnp.arange(DB)[:, None, None], ip // PAGE_SIZE]
    off = ip % PAGE_SIZE
    inew = jnp.clip(idx - past, 0, DS - 1)
    k_sel = jnp.where(in_past, cache_k_l[phys, off].astype(jnp.float32), _gather_rows(k, inew).astype(jnp.float32))
    v_sel = jnp.where(in_past, cache_v_l[phys, off].astype(jnp.float32), _gather_rows(v, inew).astype(jnp.float32))
    return attend_selected(q, k_sel, v_sel, valid)


def trunk_layer(x, lp, lb, hg_S0, rw_S0, shift0, attend):
    B, T, _ = x.shape
    h = rms_norm(x, lp['ln1'])
    p = h @ lp['w_in']
    (hq, hf, hi, hg, prw, aq, ak, av, aqi, aki, awi) = _split(
        p, [HG_HEADS * HG_DK, HG_HEADS * HG_DK, HG_W, HG_W, RW_COLS,
            AT_W, KV_W, KV_W, IDX_HEADS * IDX_DIM, IDX_DIM, IDX_HEADS])
    o_hg, hg_S = hgrn2_mixer(hq, hf, hi, hg, lb, lp['hgrn_norm'], hg_S0)
    o_rw, rw_S, shift = rwkv7_mixer(prw, shift0, lp['rwkv_mu'], lp['rwkv_w0'], lp['rwkv_w2'],
                                    lp['rwkv_a0'], lp['rwkv_a2'], lp['rwkv_g2'], lp['rwkv_kk'],
                                    lp['rwkv_ka'], lp['rwkv_rk'], lp['rwkv_lnx_w'], lp['rwkv_lnx_b'], rw_S0)
    q = rms_norm(aq.reshape(B, T, AT_HEADS, HEAD_DIM), lp['q_norm'])
    k = rms_norm(ak.reshape(B, T, AT_KV_HEADS, HEAD_DIM), lp['k_norm'])
    v = av.reshape(B, T, AT_KV_HEADS, HEAD_DIM)
    qi = aqi.reshape(B, T, IDX_HEADS, IDX_DIM)
    wi = awi * (IDX_HEADS ** -0.5 * IDX_DIM ** -0.5)
    o_at = attend(q, k, v, qi, wi, aki)
    mix = jnp.concatenate([o_hg.astype(x.dtype), o_rw.astype(x.dtype), o_at.astype(x.dtype)], axis=-1)
    x = x + mix @ lp['w_out']
    h2 = rms_norm(x, lp['ln2'])
    x = x + (jax.nn.silu(h2 @ lp['w_gate']) * (h2 @ lp['w_up'])) @ lp['w_down']
    return x, (k, v, aki, hg_S, rw_S, shift)


def setup_inputs(seed: int = 0) -> dict:
    key = jax.random.key(seed)
    ks = jax.random.split(key, 32)
    f32 = jnp.float32
    nrm = lambda i, shape, scale=1.0: jax.random.normal(ks[i], shape, f32) * scale
    gain = lambda i, shape: 1.0 + 0.02 * jax.random.normal(ks[i], shape, f32)
    n_pages = PAST_LEN // PAGE_SIZE
    n_pool = (DEC_BATCH * n_pages * 5) // 4
    perm = jax.random.permutation(ks[8], n_pool)
    page_table = perm[:DEC_BATCH * n_pages].reshape(DEC_BATCH, n_pages).astype(jnp.int32)
    return {
        'x_prompt': nrm(0, (BATCH, SEQ, D_MODEL)),
        'x_sample': nrm(1, (DEC_BATCH, DEC_SEQ, D_MODEL)),
        'cache_k': nrm(2, (DEPTH, n_pool, PAGE_SIZE, AT_KV_HEADS, HEAD_DIM)),
        'cache_v': nrm(3, (DEPTH, n_pool, PAGE_SIZE, AT_KV_HEADS, HEAD_DIM)),
        'cache_kidx': nrm(4, (DEPTH, n_pool, PAGE_SIZE, IDX_DIM)),
        'state_hgrn': nrm(5, (DEPTH, DEC_BATCH, HG_HEADS, HG_DK, HG_DV), 0.5),
        'state_rwkv': nrm(6, (DEPTH, DEC_BATCH, RW_HEADS, RW_HD, RW_HD), 0.5),
        'state_shift': nrm(7, (DEPTH, DEC_BATCH, RW_COLS)),
        'page_table': page_table,
        'ln1': gain(9, (DEPTH, D_MODEL)),
        'w_in': nrm(10, (DEPTH, D_MODEL, IN_COLS), D_MODEL ** -0.5),
        'hgrn_lb': nrm(11, (DEPTH, HG_HEADS * HG_DK)),
        'hgrn_norm': gain(12, (DEPTH, HG_W)),
        'rwkv_mu': jax.random.uniform(ks[13], (DEPTH, RW_COLS), f32),
        'rwkv_w0': nrm(14, (DEPTH, RW_W), 0.5) - 1.0,
        'rwkv_w2': nrm(15, (DEPTH, RW_DECAY_LORA, RW_W), 0.5 * RW_DECAY_LORA ** -0.5),
        'rwkv_a0': nrm(16, (DEPTH, RW_W), 0.1),
        'rwkv_a2': nrm(17, (DEPTH, RW_AAA_LORA, RW_W), 0.5 * RW_AAA_LORA ** -0.5),
        'rwkv_g2': nrm(18, (DEPTH, RW_GATE_LORA, RW_W), RW_GATE_LORA ** -0.5),
        'rwkv_kk': 0.85 + nrm(19, (DEPTH, RW_W), 0.05),
        'rwkv_ka': 1.0 + nrm(20, (DEPTH, RW_W), 0.05),
        'rwkv_rk': nrm(21, (DEPTH, RW_HEADS, RW_HD), 0.1),
        'rwkv_lnx_w': gain(22, (DEPTH, RW_W)),
        'rwkv_lnx_b': nrm(23, (DEPTH, RW_W), 0.02),
        'q_norm': gain(24, (DEPTH, HEAD_DIM)),
        'k_norm': gain(25, (DEPTH, HEAD_DIM)),
        'w_out': nrm(26, (DEPTH, D_MODEL, D_MODEL), D_MODEL ** -0.5),
        'ln2': gain(27, (DEPTH, D_MODEL)),
        'w_gate': nrm(28, (DEPTH, D_MODEL, D_FF), D_MODEL ** -0.5),
        'w_up': nrm(29, (DEPTH, D_MODEL, D_FF), D_MODEL ** -0.5),
        'w_down': nrm(30, (DEPTH, D_FF, D_MODEL), D_FF ** -0.5),
    }


def reference(x_prompt, x_sample, cache_k, cache_v, cache_kidx, state_hgrn, state_rwkv, state_shift,
              page_table, ln1, w_in, hgrn_lb, hgrn_norm, rwkv_mu, rwkv_w0, rwkv_w2, rwkv_a0, rwkv_a2,
              rwkv_g2, rwkv_kk, rwkv_ka, rwkv_rk, rwkv_lnx_w, rwkv_lnx_b, q_norm, k_norm, w_out, ln2,
              w_gate, w_up, w_down):
    f32 = jnp.float32
    lb_p = jax.nn.softmax(hgrn_lb.astype(f32), axis=0)
    lbs = jnp.cumsum(lb_p, axis=0) - lb_p[0:1]
    layers = [dict(ln1=ln1[l], w_in=w_in[l], hgrn_norm=hgrn_norm[l], rwkv_mu=rwkv_mu[l],
                   rwkv_w0=rwkv_w0[l], rwkv_w2=rwkv_w2[l], rwkv_a0=rwkv_a0[l], rwkv_a2=rwkv_a2[l],
                   rwkv_g2=rwkv_g2[l], rwkv_kk=rwkv_kk[l], rwkv_ka=rwkv_ka[l], rwkv_rk=rwkv_rk[l],
                   rwkv_lnx_w=rwkv_lnx_w[l], rwkv_lnx_b=rwkv_lnx_b[l], q_norm=q_norm[l], k_norm=k_norm[l],
                   w_out=w_out[l], ln2=ln2[l], w_gate=w_gate[l], w_up=w_up[l], w_down=w_down[l])
              for l in range(DEPTH)]

    def run(x, hg0, rw0, sh0, attend_for_layer):
        outs = []
        for l in range(DEPTH):
            x, st = trunk_layer(x, layers[l], lbs[l], hg0[l], rw0[l], sh0[l], attend_for_layer(l))
            outs.append(st)
        return x, [jnp.stack([o[i] for o in outs]) for i in range(6)]

    B = x_prompt.shape[0]
    hg0_p = jnp.zeros((DEPTH, B, HG_HEADS, HG_DK, HG_DV), f32)
    rw0_p = jnp.zeros((DEPTH, B, RW_HEADS, RW_HD, RW_HD), f32)
    sh0_p = jnp.zeros((DEPTH, B, RW_COLS), f32)
    y_prompt, st_p = run(x_prompt, hg0_p, rw0_p, sh0_p, lambda l: dsa_prompt)
    k_p, v_p, kidx_p, hg_p, rw_p, sh_p = st_p

    sample_attend = lambda l: functools.partial(dsa_sample, cache_k_l=cache_k[l], cache_v_l=cache_v[l],
                                                cache_kidx_l=cache_kidx[l], page_table=page_table)
    y_sample, st_s = run(x_sample, state_hgrn, state_rwkv, state_shift, sample_attend)
    k_s, v_s, kidx_s, hg_s, rw_s, sh_s = st_s

    return (y_prompt, y_sample, k_p, v_p, kidx_p, hg_p, rw_p, sh_p, k_s, v_s, kidx_s, hg_s, rw_s, sh_s)
```

```python
import numpy as np
import concourse.bass as bass
import concourse.mybir as mybir

F32 = mybir.dt.float32
BF16 = mybir.dt.bfloat16
I32 = mybir.dt.int32
ALU = mybir.AluOpType
AF = mybir.ActivationFunctionType
AX = mybir.AxisListType

COMPUTE = ("pe", "act", "dve", "pool")
NDSEM = 6


class T:
    def __init__(self, h, name):
        self.h = h
        self.name = name
        self.lw = None
        self.rd = []
        self.kids = {}
        self.parent = None
        self.excl = False

    def sub(self, key):
        k = self.kids.get(key)
        if k is None:
            k = T(self.h, f"{self.name}.{key}")
            k.parent = self
            self.kids[key] = k
        return k

    def __getitem__(self, idx):
        return self.h[idx]


class Prog:
    def __init__(self, nc):
        self.nc = nc
        self.ops = []
        self.n = 0

    def _deps(self, rd, wr, idx):
        deps = set()

        def nodes(t):
            if t.parent is not None:
                return [t], [t.parent]
            return [t] + list(t.kids.values()), []

        for t in rd:
            own, look = nodes(t)
            for n in own + look:
                if n.lw is not None:
                    deps.add(n.lw)
        for t in wr:
            own, look = nodes(t)
            for n in own + look:
                if n.lw is not None:
                    deps.add(n.lw)
                deps.update(n.rd)
        for t in rd:
            own, _ = nodes(t)
            for n in own:
                n.rd.append(idx)
        for t in wr:
            own, _ = nodes(t)
            for n in own:
                n.lw = idx
                n.rd = []
        deps.discard(idx)
        return deps

    def op(self, eng, fn, rd=(), wr=(), dma=False):
        idx = len(self.ops)
        rd = list(rd)
        wr = list(wr) + [t for t in rd if t.excl]
        rd = [t for t in rd if not t.excl]
        deps = self._deps(rd, wr, idx)
        self.ops.append(dict(eng=eng, fn=fn, deps=deps, dma=dma, inc=False))
        return idx

    def emit(self):
        nc = self.nc
        ops = self.ops
        engs = ["pe", "act", "dve", "pool", "sp"]
        for i, o in enumerate(ops):
            for d in o["deps"]:
                p = ops[d]
                if p["dma"]:
                    continue
                if p["eng"] == o["eng"] and o["eng"] == "pe":
                    continue
                p["inc"] = True
        import contextlib
        with contextlib.ExitStack() as st:
            csem = {e: st.enter_context(nc.semaphore(f"c_{e}")) for e in engs}
            dsem = {e: [st.enter_context(nc.semaphore(f"d_{e}{j}")) for j in range(NDSEM)]
                    for e in ("sp", "pool", "act")}
            ccount = {e: 0 for e in engs}
            dcount = {e: 0 for e in dsem}
            for o in ops:
                e = o["eng"]
                if o["dma"]:
                    n = dcount[e]
                    dcount[e] += 1
                    j = n % NDSEM
                    o["sem"] = dsem[e][j]
                    o["val"] = 16 * (n // NDSEM + 1)
                    o["prev"] = (dsem[e][j], 16 * (n // NDSEM)) if n >= NDSEM else None
                elif o["inc"]:
                    ccount[e] += 1
                    o["sem"] = csem[e]
                    o["val"] = ccount[e]
            waited = {e: {} for e in engs}
            for o in ops:
                e = o["eng"]
                w = {}
                for d in o["deps"]:
                    p = ops[d]
                    if (not p["dma"]) and p["eng"] == e and e == "pe":
                        continue
                    s, v = p["sem"], p["val"]
                    k = id(s)
                    if w.get(k, (None, 0))[1] < v:
                        w[k] = (s, v)
                if o["dma"] and o["prev"] is not None:
                    s, v = o["prev"]
                    k = id(s)
                    if w.get(k, (None, 0))[1] < v:
                        w[k] = (s, v)
                waits = []
                for k, (s, v) in w.items():
                    if waited[e].get(k, 0) < v:
                        waited[e][k] = v
                        waits.append((s, v))
                o["waits"] = waits
            finals = []
            for e in dsem:
                n = dcount[e]
                for j in range(min(n, NDSEM)):
                    cnt = (n - j + NDSEM - 1) // NDSEM
                    finals.append((dsem[e][j], 16 * cnt))
            block = st.enter_context(nc.Block())

            def run(eng_name, engine):
                for o in ops:
                    if o["eng"] != eng_name:
                        continue
                    for s, v in o["waits"]:
                        engine.wait_ge(s, v)
                    inst = o["fn"](engine)
                    if o["dma"]:
                        inst.then_inc(o["sem"], 16)
                    elif o["inc"]:
                        inst.then_inc(o["sem"], 1)
                if eng_name == "sp":
                    for s, v in finals:
                        engine.wait_ge(s, v)

            @block.tensor
            def _(e):
                run("pe", e)

            @block.scalar
            def _(e):
                run("act", e)

            @block.vector
            def _(e):
                run("dve", e)

            @block.gpsimd
            def _(e):
                run("pool", e)

            @block.sync
            def _(e):
                run("sp", e)
        return nc

import contextlib
import ml_dtypes
from concourse.bass_utils import run_bass_kernel_spmd


class V:
    def __init__(self, t, ap):
        self.t = t
        self.ap = ap

    def __getitem__(self, idx):
        return V(self.t, self.ap[idx])

    def re(self, pat, **kw):
        return V(self.t, self.ap.rearrange(pat, **kw))

    def bc(self, shape, axis):
        return V(self.t, self.ap.unsqueeze(axis).to_broadcast(shape))


def tv(t, idx=None):
    ap = t.h[:] if idx is None else t.h[idx]
    return V(t, ap)


class Cfg:
    def __init__(s, D=4096, T=2048, TS=4, PAST=16384, HGH=8, RWH=16, ATH=16, KVH=4, IDXH=8,
                 TOPK=256, DEPTH=2, NPOOL=1280, DFF=None):
        s.D, s.T, s.TS, s.PAST, s.HGH, s.RWH, s.ATH, s.KVH, s.IDXH = D, T, TS, PAST, HGH, RWH, ATH, KVH, IDXH
        s.TOPK, s.DEPTH, s.NPOOL = TOPK, DEPTH, NPOOL
        s.HGW, s.RWW, s.ATW, s.KVW = HGH * 128, RWH * 64, ATH * 128, KVH * 128
        assert s.HGW + s.RWW + s.ATW == D
        s.RWC = 3 * s.RWW + 64 + 64 + 160
        s.c_hq, s.c_hf, s.c_hi, s.c_hg = 0, s.HGW, 2 * s.HGW, 3 * s.HGW
        s.c_rw = 4 * s.HGW
        s.c_aq = s.c_rw + s.RWC
        s.c_ak = s.c_aq + s.ATW
        s.c_av = s.c_ak + s.KVW
        s.c_aqi = s.c_av + s.KVW
        s.c_aki = s.c_aqi + IDXH * 128
        s.c_awi = s.c_aki + 128
        s.INC = s.c_awi + IDXH
        s.DFF = DFF if DFF is not None else -(-8 * D // (3 * 256)) * 256
        s.NT = T + TS
        s.NPAGES = PAST // 128
        s.KC = D // 128
        s.FC = s.DFF // 128
        s.TB = min(512, T)
        s.CH = min(64, T)


class KB:
    def __init__(s, cfg):
        s.c = cfg
        s.nc = bass.Bass("TRN2", target_bir_lowering=False)
        s.P = Prog(s.nc)
        s.st = contextlib.ExitStack()
        s.din = {}
        s.dout = {}
        s.rr = 0

    def dram(s, name, shape, dt=F32, kind="Internal"):
        h = s.nc.dram_tensor(name, list(shape), dt, kind=kind).ap()
        t = T(h, name)
        if kind == "ExternalInput":
            s.din[name] = (tuple(shape), dt)
        if kind == "ExternalOutput":
            s.dout[name] = (tuple(shape), dt)
        return t

    def sb_raw(s, name, shape, dt):
        return T(s.st.enter_context(s.nc.sbuf_tensor(name, list(shape), dt)), name)

    def ps_raw(s, name, shape, dt):
        t = T(s.st.enter_context(s.nc.psum_tensor(name, list(shape), dt)), name)
        t.excl = True
        return t

    def phase(s, name):
        s.barrier()
        s.o32 = 0
        s.o16 = 0
        s.pname = name
        s.pcount = getattr(s, "pcount", 0) + 1

    def a32(s, n, name="t", parts=128):
        assert s.o32 + n <= s.A32, (s.pname, name, s.o32, n)
        t = T(s.arena32.h[0:parts, s.o32:s.o32 + n], f"{s.pname}{s.pcount}.{name}")
        s.o32 += (n + 15) // 16 * 16
        return t

    def a16(s, n, name="t", parts=128):
        assert s.o16 + n <= s.A16, (s.pname, name, s.o16, n)
        t = T(s.arena16.h[0:parts, s.o16:s.o16 + n], f"{s.pname}{s.pcount}.{name}")
        s.o16 += (n + 15) // 16 * 16
        return t

    def barrier(s):
        P = s.P
        idx = len(P.ops)
        deps = set(getattr(s, "_since", []))
        P.ops.append(dict(eng="dve", fn=lambda e: e.memset(s.bar.h[0:1, 0:1], 0.0), deps=deps, dma=False, inc=False))
        s._since = [idx]
        s._bar = idx

    def op(s, eng, fn, rd=(), wr=(), dma=False):
        if getattr(s, "_lim", None) is not None:
            s._cnt = getattr(s, "_cnt", 0) + 1
            if s._cnt > s._lim:
                return None
        i = s.P.op(eng, fn, [v.t if isinstance(v, V) else v for v in rd], [v.t if isinstance(v, V) else v for v in wr], dma)
        if getattr(s, "_bar", None) is not None:
            s.P.ops[i]["deps"].add(s._bar)
        s._since.append(i)
        return i

    def MM(s, out, lhsT, rhs, start=True, stop=True):
        s.op("pe", lambda e: e.matmul(out.ap, lhsT.ap, rhs.ap, start=start, stop=stop), rd=[lhsT, rhs], wr=[out])

    def TR(s, out, in_, ident):
        s.op("pe", lambda e: e.transpose(out.ap, in_.ap, ident.ap), rd=[in_, ident], wr=[out])

    def ACT(s, out, in_, func, bias=None, scale=1.0, accum=None):
        rd = [in_] + ([bias] if isinstance(bias, V) else []) + ([scale] if isinstance(scale, V) else [])
        wr = [out] + ([accum] if accum is not None else [])
        kw = {}
        if bias is not None:
            kw["bias"] = bias.ap if isinstance(bias, V) else bias
        if accum is not None:
            kw["accum_out"] = accum.ap
        sc = scale.ap if isinstance(scale, V) else scale
        s.op("act", lambda e: e.activation(out=out.ap, in_=in_.ap, func=func, scale=sc, **kw), rd=rd, wr=wr)

    def TT(s, out, a, b, op, eng="dve"):
        s.op(eng, lambda e: e.tensor_tensor(out=out.ap, in0=a.ap, in1=b.ap, op=op), rd=[a, b], wr=[out])

    def TS(s, out, a, s1, op0, s2=None, op1=None, accum=None, eng="dve"):
        rd = [a] + [x for x in (s1, s2) if isinstance(x, V)]
        wr = [out] + ([accum] if accum is not None else [])
        v1 = s1.ap if isinstance(s1, V) else s1
        v2 = s2.ap if isinstance(s2, V) else s2
        kw = {}
        if op1 is not None:
            kw["op1"] = op1
        if accum is not None:
            kw["accum_out"] = accum.ap
        s.op(eng, lambda e: e.tensor_scalar(out=out.ap, in0=a.ap, scalar1=v1, scalar2=v2, op0=op0, **kw), rd=rd, wr=wr)

    def STT(s, out, a, sc, b, op0, op1, eng="dve"):
        rd = [a, b] + ([sc] if isinstance(sc, V) else [])
        v = sc.ap if isinstance(sc, V) else sc
        s.op(eng, lambda e: e.scalar_tensor_tensor(out=out.ap, in0=a.ap, scalar=v, in1=b.ap, op0=op0, op1=op1), rd=rd, wr=[out])

    def CP(s, out, in_, eng=None):
        if eng is None:
            s.rr += 1
            eng = "act" if s.rr % 2 else "dve"
        if eng == "act":
            s.op("act", lambda e: e.activation(out=out.ap, in_=in_.ap, func=AF.Copy), rd=[in_], wr=[out])
        else:
            s.op(eng, lambda e: e.tensor_copy(out=out.ap, in_=in_.ap), rd=[in_], wr=[out])

    def MS(s, out, val, eng="dve"):
        s.op(eng, lambda e: e.memset(out.ap, val), wr=[out])

    def RCP(s, out, in_):
        s.op("dve", lambda e: e.reciprocal(out=out.ap, in_=in_.ap), rd=[in_], wr=[out])

    def RSUM(s, out, in_):
        s.op("dve", lambda e: e.reduce_sum(out=out.ap, in_=in_.ap, axis=AX.X), rd=[in_], wr=[out])

    def DMA(s, out, in_, q="sp"):
        s.op(q, lambda e: e.dma_start(out=out.ap, in_=in_.ap), rd=[in_], wr=[out], dma=True)

    def DMAs(s, out, in_, q="sp"):
        def f(e):
            with s.nc.allow_non_contiguous_dma(reason="small strided"):
                return e.dma_start(out=out.ap, in_=in_.ap)
        s.op(q, f, rd=[in_], wr=[out], dma=True)

    def psum(s, bf=False):
        if bf:
            s.pbi = (getattr(s, "pbi", -1) + 1) % len(s.psb)
            return s.psb[s.pbi]
        s.pfi = (getattr(s, "pfi", -1) + 1) % len(s.psf)
        return s.psf[s.pfi]


class KB2(KB):
    def setup(s):
        c = s.c
        s.A32, s.A16 = 19968, 60416
        s.arena32 = s.sb_raw("arena32", [128, s.A32], F32)
        s.arena16 = s.sb_raw("arena16", [128, s.A16], BF16)
        s.bar = s.sb_raw("bar", [128, 8], F32)
        s.identb = s.sb_raw("identb", [128, 128], BF16)
        s.identf = s.sb_raw("identf", [128, 128], F32)
        s.psf = [s.ps_raw(f"psf{i}", [128, 512], F32) for i in range(6)]
        s.psb = [s.ps_raw(f"psb{i}", [128, 1024], BF16) for i in range(2)]
        IN = "ExternalInput"
        OUT = "ExternalOutput"
        L = c.DEPTH
        s.x_in = s.dram("x_in", [c.NT, c.D], F32, IN)
        s.d_identb = s.dram("c_identb", [128, 128], BF16, IN)
        s.d_identf = s.dram("c_identf", [128, 128], F32, IN)
        s.w_in = s.dram("w_in", [L, c.D, c.INC], F32, IN)
        s.w_out = s.dram("w_out", [L, c.D, c.D], F32, IN)
        s.w_gate = s.dram("w_gate", [L, c.D, c.DFF], F32, IN)
        s.w_up = s.dram("w_up", [L, c.D, c.DFF], F32, IN)
        s.w_down = s.dram("w_down", [L, c.DFF, c.D], F32, IN)
        s.ln1 = s.dram("ln1", [L, c.D], F32, IN)
        s.ln2 = s.dram("ln2", [L, c.D], F32, IN)
        s.xres = s.dram("xres", [c.NT, c.D], F32, OUT)
        s.pT = s.dram("pT", [c.INC, c.NT], F32)
        s.mixT = s.dram("mixT", [c.D, c.NT], BF16)
        s._since = []
        s._bar = None
        s.barrier()
        s.DMA(tv(s.identb), tv(s.d_identb))
        s.DMA(tv(s.identf), tv(s.d_identf))
        s.blocks = []
        nb = c.T // c.TB
        for b in range(nb):
            toks = [(0, c.TB, b * c.TB)]
            if b == nb - 1:
                toks.append((c.TB, c.TS, c.T))
            s.blocks.append(toks)
        s.BW = c.TB + c.TS

    @staticmethod
    def tiles128(toks):
        out = []
        for (c0, n, g0) in toks:
            for o in range(0, n, 128):
                m = min(128, n - o)
                out.append((c0 + o, m, g0 + o))
        return out

    def norm_block(s, xsrc, toks, grow, actT, bufs):
        c = s.c
        xts, sq, xn, ss = bufs
        for i, (c0, n, g0) in enumerate(s.tiles128(toks)):
            xt = xts[i % 2]
            s.DMA(tv(xt)[:n], tv(xsrc)[g0:g0 + n, :])
            s.TT(tv(sq)[:n], tv(xt)[:n], tv(xt)[:n], ALU.mult)
            s.RSUM(tv(ss)[:n, 0:1], tv(sq)[:n])
            s.TS(tv(ss)[:n, 1:2], tv(ss)[:n, 0:1], 1.0 / c.D, ALU.mult, 1e-6, ALU.add)
            s.ACT(tv(ss)[:n, 2:3], tv(ss)[:n, 1:2], AF.Ln)
            s.ACT(tv(ss)[:n, 3:4], tv(ss)[:n, 2:3], AF.Exp, scale=-0.5)
            s.STT(tv(xn)[:n], tv(xt)[:n], tv(ss)[:n, 3:4], tv(grow)[:n], ALU.mult, ALU.mult)
            for k0 in range(0, c.KC, 8):
                kk = min(8, c.KC - k0)
                pb = s.psum(bf=True)
                for j in range(kk):
                    s.TR(tv(pb)[:, j * 128:j * 128 + n], tv(xn)[:n, (k0 + j) * 128:(k0 + j + 1) * 128], tv(s.identb)[:n, :n])
                s.CP(actT[:, k0:k0 + kk, c0:c0 + n], tv(pb)[:, 0:kk * 128].re("p (k t) -> p k t", k=kk)[:, :, 0:n])

    @staticmethod
    def fm_chunks(toks):
        merged = []
        for (c0, n, g0) in toks:
            if merged and merged[-1][0] + merged[-1][1] == c0 and merged[-1][2] + merged[-1][1] == g0:
                merged[-1] = (merged[-1][0], merged[-1][1] + n, merged[-1][2])
            else:
                merged.append((c0, n, g0))
        out = []
        for (c0, n, g0) in merged:
            k = -(-n // 512)
            base, rem = divmod(n, k)
            o = 0
            for i in range(k):
                m = base + (1 if i < rem else 0)
                out.append((c0 + o, m, g0 + o))
                o += m
        return out

    def wtile(s):
        s.wi = getattr(s, "wi", -1) + 1
        return s.wts[s.wi % len(s.wts)]

    def dense_fm(s, actT, toks, Wd, KC, col_lo, col_hi, epi):
        Wv = Wd.ap.rearrange("(kc p) c -> p kc c", p=128)
        for j0 in range(col_lo, col_hi, 128):
            cw = min(128, col_hi - j0)
            wt = s.wtile()
            wv = tv(wt).re("p (k c) -> p k c", c=128)
            s.DMA(wv[:, 0:KC, 0:cw], V(Wd.t, Wv[:, :, j0:j0 + cw]), q="pool")
            for (c0, n, g0) in s.fm_chunks(toks):
                ps = s.psum()
                for kc in range(KC):
                    s.MM(tv(ps)[:cw, :n], wv[:, kc, 0:cw], actT[:, kc, c0:c0 + n], start=(kc == 0), stop=(kc == KC - 1))
                epi(j0, cw, c0, n, g0, ps)

    def dense_tm(s, actT, toks, Wd, k_lo, nkc, epi):
        ncols = Wd.ap.shape[1]
        Wv = Wd.ap.rearrange("(kc p) c -> p kc c", p=128)
        tt = s.tiles128(toks)
        assert len(tt) <= 5
        for cb0 in range(0, ncols, 512):
            cb = min(512, ncols - cb0)
            pss = [s.psum() for _ in tt]
            for k0 in range(0, nkc, 8):
                kn = min(8, nkc - k0)
                wt = s.wtile()
                wv = tv(wt).re("p (k c) -> p k c", c=512)
                s.DMA(wv[:, 0:kn, 0:cb], V(Wd.t, Wv[:, k_lo + k0:k_lo + k0 + kn, cb0:cb0 + cb]), q="pool")
                for ti, (c0, n, g0) in enumerate(tt):
                    for kk in range(kn):
                        s.MM(tv(pss[ti])[:n, :cb], actT[:, k0 + kk, c0:c0 + n], wv[:, kk, 0:cb],
                             start=(k0 + kk == 0), stop=(k0 + kk == nkc - 1))
            for ti, (c0, n, g0) in enumerate(tt):
                epi(g0, n, cb0, cb, pss[ti])

    def resid_epi(s, stg):
        def epi(g0, n, cb0, cb, ps):
            s.sti = getattr(s, "sti", -1) + 1
            r = stg[s.sti % len(stg)]
            xr = s.xres.sub(g0)
            s.DMA(tv(r)[:n, :cb], tv(xr)[g0:g0 + n, cb0:cb0 + cb])
            s.TT(tv(r)[:n, :cb], tv(r)[:n, :cb], tv(ps)[:n, :cb], ALU.add)
            s.DMA(tv(xr)[g0:g0 + n, cb0:cb0 + cb], tv(r)[:n, :cb])
        return epi

    def grow_load(s, dst, src_row):
        s.DMA(tv(dst), V(src_row.t, src_row.ap.partition_broadcast(128)))

    def big_blocks(s, TBB):
        c = s.c
        TBB = min(TBB, c.T)
        nb = c.T // TBB
        out = []
        for b in range(nb):
            toks = [(o, min(512, TBB - o), b * TBB + o) for o in range(0, TBB, 512)]
            if b == nb - 1:
                toks.append((TBB, c.TS, c.T))
            out.append(toks)
        return out, TBB + c.TS

    def phase_win(s, l):
        c = s.c
        xsrc = s.x_in if l == 0 else s.xres
        blocks, BWw = s.big_blocks(1024)
        for bi, toks in enumerate(blocks):
            s.phase("win")
            actT = tv(s.a16(c.KC * BWw, "actT")).re("p (k t) -> p k t", t=BWw)
            s.wts = [s.a16(4096, f"w{i}") for i in range(3)]
            xn = s.a16(c.D, "xn")
            xts = [s.a32(c.D, f"xt{i}") for i in range(2)]
            sq = s.a32(c.D, "sq")
            grow = s.a32(c.D, "grow")
            ss = s.a32(8, "ss")
            stg = [s.a32(512, f"stg{i}") for i in range(3)]
            s.grow_load(grow, tv(s.ln1)[l])
            s.norm_block(xsrc, toks, grow, actT, (xts, sq, xn, ss))

            def epi(j0, cw, c0, n, g0, ps):
                s.sti = getattr(s, "sti", -1) + 1
                r = stg[s.sti % len(stg)]
                s.CP(tv(r)[:cw, :n], tv(ps)[:cw, :n])
                s.DMA(tv(s.pT.sub((j0, g0)))[j0:j0 + cw, g0:g0 + n], tv(r)[:cw, :n])
            s.dense_fm(actT, toks, tv(s.w_in)[l], c.KC, 0, c.INC, epi)

    def phase_wout(s, l):
        c = s.c
        for bi, toks in enumerate(s.blocks):
            s.phase("wout")
            actT = tv(s.a16(c.KC * s.BW, "actT")).re("p (k t) -> p k t", t=s.BW)
            s.wts = [s.a16(4096, f"w{i}") for i in range(3)]
            stg = [s.a32(512, f"stg{i}") for i in range(4)]
            mv = s.mixT.h.rearrange("(k p) t -> p k t", p=128)
            for (c0, n, g0) in toks:
                s.DMA(actT[:, :, c0:c0 + n], V(s.mixT, mv[:, :, g0:g0 + n]))
            if l == 0 and bi == 0:
                pass
            s.dense_tm(actT, toks, tv(s.w_out)[l], 0, c.KC, s.resid_epi(stg))

    def phase_ffn(s, l):
        c = s.c
        nh = 2
        fh = c.FC // nh
        assert fh * nh == c.FC
        for bi, toks in enumerate(s.blocks):
            s.phase("ffn")
            actT = tv(s.a16(c.KC * s.BW, "actT")).re("p (k t) -> p k t", t=s.BW)
            hidT = tv(s.a16(fh * s.BW, "hidT")).re("p (k t) -> p k t", t=s.BW)
            s.wts = [s.a16(4096, f"w{i}") for i in range(4)]
            xn = s.a16(c.D, "xn")
            xts = [s.a32(c.D, f"xt{i}") for i in range(2)]
            sq = s.a32(c.D, "sq")
            grow = s.a32(c.D, "grow")
            ss = s.a32(8, "ss")
            stg = [s.a32(512, f"stg{i}") for i in range(4)]
            sgs = [s.a32(512, f"sg{i}") for i in range(2)]
            s.grow_load(grow, tv(s.ln2)[l])
            s.norm_block(s.xres, toks, grow, actT, (xts, sq, xn, ss))
            Wg = tv(s.w_gate)[l]
            Wu = tv(s.w_up)[l]
            Wgv = Wg.ap.rearrange("(kc p) c -> p kc c", p=128)
            Wuv = Wu.ap.rearrange("(kc p) c -> p kc c", p=128)
            for h in range(nh):
                for fj in range(fh):
                    f0 = (h * fh + fj) * 128
                    wg = tv(s.wtile()).re("p (k c) -> p k c", c=128)
                    wu = tv(s.wtile()).re("p (k c) -> p k c", c=128)
                    s.DMA(wg[:, 0:c.KC, :], V(s.w_gate, Wgv[:, :, f0:f0 + 128]), q="pool")
                    s.DMA(wu[:, 0:c.KC, :], V(s.w_up, Wuv[:, :, f0:f0 + 128]), q="pool")
                    for (c0, n, g0) in s.fm_chunks(toks):
                        pg = s.psum()
                        pu = s.psum()
                        for kc in range(c.KC):
                            s.MM(tv(pg)[:, :n], wg[:, kc, :], actT[:, kc, c0:c0 + n], start=(kc == 0), stop=(kc == c.KC - 1))
                        for kc in range(c.KC):
                            s.MM(tv(pu)[:, :n], wu[:, kc, :], actT[:, kc, c0:c0 + n], start=(kc == 0), stop=(kc == c.KC - 1))
                        s.sgi = getattr(s, "sgi", -1) + 1
                        sg = sgs[s.sgi % 2]
                        s.ACT(tv(sg)[:, :n], tv(pg)[:, :n], AF.Silu)
                        s.TT(hidT[:, fj, c0:c0 + n], tv(sg)[:, :n], tv(pu)[:, :n], ALU.mult)
                s.dense_tm(hidT, toks, tv(s.w_down)[l], h * fh, fh, s.resid_epi(stg))


def np_dt(dt):
    return {F32: np.float32, BF16: ml_dtypes.bfloat16, I32: np.int32}[dt]


class KB3(KB2):
    def chunk_la(s, arr, gC, S_f, S_b, yT, n, C, hp, delta, bufs, mask):
        dh = 128 // hp
        nch = n // C
        rt, kt, kh, vT = arr["rt"], arr["kt"], arr["kh"], arr["vT"]
        NM = 5 if delta else 1
        Mh, tok, Vp, Up, R_f, R_b, Xs = bufs["Mh"], bufs["tok"], bufs["Vp"], bufs["Up"], bufs["R_f"], bufs["R_b"], bufs["Xs"]
        nlev = int(np.log2(C)) if C > 1 else 0
        for ci in range(nch):
            cs = slice(ci * C, (ci + 1) * C)
            for h in range(hp):
                hs = slice(h * dh, (h + 1) * dh)
                pm = s.psum()
                if delta:
                    at, bt = arr["at"], arr["bt"]
                    pairs = [(bt, at), (bt, rt), (kt, at), (kt, rt), (at, bt)]
                else:
                    pairs = [(kt, rt)]
                for i, (lt, rh) in enumerate(pairs):
                    s.MM(tv(pm)[:C, i * C:(i + 1) * C], lt[hs, cs], rh[hs, cs])
                s.TT(tv(Mh[h])[:C, :NM * C].re("p (m c) -> p m c", c=C), tv(pm)[:C, :NM * C].re("p (m c) -> p m c", c=C),
                     mask[:C, :, :C], ALU.mult)
            yield
            pb = s.psum(bf=True)
            srcs = ([arr["bh"]] if delta else []) + [kh, vT]
            for i, a in enumerate(srcs):
                s.TR(tv(pb)[:C, i * 128:(i + 1) * 128], a[:, cs], tv(s.identb))
            nt = len(srcs)
            s.CP(tv(tok)[:C, :nt * 128], tv(pb)[:C, :nt * 128])
            tk = tv(tok)[:C, (nt - 2) * 128:(nt - 1) * 128]
            tvv = tv(tok)[:C, (nt - 1) * 128:nt * 128]
            tb = tv(tok)[:C, 0:128] if delta else None
            if hp > 1:
                for h in range(hp):
                    hs = slice(h * dh, (h + 1) * dh)
                    s.CP(tv(Vp[h])[:C, hs], tvv[:, hs])
                vps = [tv(Vp[h])[:C, :] for h in range(hp)]
            else:
                vps = [tvv]
            yield
            if delta:
                pr = s.psum()
                for h in range(hp):
                    hs = slice(h * dh, (h + 1) * dh)
                    s.MM(tv(pr)[:C, hs], arr["at"][:, cs], tv(S_b)[:, hs], start=True, stop=False)
                    s.MM(tv(pr)[:C, hs], tv(Mh[h])[:C, 2 * C:3 * C], tvv[:, hs], start=False, stop=True)
                s.CP(tv(R_f)[:C, :], tv(pr)[:C, 0:128], eng="dve")
                s.CP(tv(R_b)[:C, :], tv(pr)[:C, 0:128], eng="act")
                yield
                X = [tv(Mh[h])[:C, 4 * C:5 * C] for h in range(hp)]
                XT = [tv(Mh[h])[:C, 0:C] for h in range(hp)]
                for lev in range(nlev):
                    pa = s.psum()
                    for h in range(hp):
                        hs = slice(h * dh, (h + 1) * dh)
                        s.MM(tv(pa)[:C, hs], XT[h], tv(R_b)[:C, hs])
                    px = None
                    if lev < nlev - 1:
                        px = s.psum()
                        for h in range(hp):
                            s.MM(tv(px)[:C, (2 * h) * C:(2 * h + 1) * C], XT[h], X[h])
                            s.MM(tv(px)[:C, (2 * h + 1) * C:(2 * h + 2) * C], X[h], XT[h])
                    s.TT(tv(R_f)[:C, :], tv(R_f)[:C, :], tv(pa)[:C, 0:128], ALU.add)
                    s.CP(tv(R_b)[:C, :], tv(R_f)[:C, :], eng="act")
                    if px is not None:
                        xs = Xs[lev % 2]
                        s.CP(tv(xs)[:C, :2 * hp * C], tv(px)[:C, :2 * hp * C], eng="dve")
                        X = [tv(xs)[:C, (2 * h) * C:(2 * h + 1) * C] for h in range(hp)]
                        XT = [tv(xs)[:C, (2 * h + 1) * C:(2 * h + 2) * C] for h in range(hp)]
                    yield
                if hp > 1:
                    for h in range(hp):
                        hs = slice(h * dh, (h + 1) * dh)
                        s.CP(tv(Up[h])[:C, hs], tv(R_b)[:C, hs])
                    ups = [tv(Up[h])[:C, :] for h in range(hp)]
                else:
                    ups = [tv(R_b)[:C, :]]
            py = s.psum()
            s.MM(tv(py)[:, :C], tv(S_b), rt[:, cs], start=True, stop=False)
            for h in range(hp):
                last = (h == hp - 1)
                if delta:
                    s.MM(tv(py)[:, :C], ups[h], tv(Mh[h])[:C, 1 * C:2 * C], start=False, stop=False)
                    s.MM(tv(py)[:, :C], vps[h], tv(Mh[h])[:C, 3 * C:4 * C], start=False, stop=last)
                else:
                    s.MM(tv(py)[:, :C], vps[h], tv(Mh[h])[:C, 0:C], start=False, stop=last)
            s.CP(yT[:, cs], tv(py)[:, :C])
            yield
            pS = s.psum()
            for h in range(hp):
                hs = slice(h * dh, (h + 1) * dh)
                if delta:
                    s.MM(tv(pS)[:, hs], tb, tv(R_b)[:C, hs], start=True, stop=False)
                    s.MM(tv(pS)[:, hs], tk, tvv[:, hs], start=False, stop=True)
                else:
                    s.MM(tv(pS)[:, hs], tk, tvv[:, hs], start=True, stop=True)
            for h in range(hp):
                hs = slice(h * dh, (h + 1) * dh)
                s.STT(tv(S_f)[hs, :], tv(S_f)[hs, :], gC[hs, ci:ci + 1], tv(pS)[hs, hs], ALU.mult, ALU.add)
                s.CP(tv(S_b)[hs, hs], tv(S_f)[hs, :], eng="act")
            yield

    def run_interleaved(s, gens):
        gens = list(gens)
        while gens:
            nxt = []
            for g in gens:
                try:
                    next(g)
                    nxt.append(g)
                except StopIteration:
                    pass
            gens = nxt

    def chunk_cumsum(s, a, b, n, C):
        d = 1
        while d < C:
            av = tv(a)[:, :n].re("p (c t) -> p c t", t=C)
            bv = tv(b)[:, :n].re("p (c t) -> p c t", t=C)
            s.TT(bv[:, :, d:], av[:, :, d:], av[:, :, :C - d], ALU.add)
            s.CP(bv[:, :, :d], av[:, :, :d], eng="dve")
            a, b = b, a
            d *= 2
        return a, b

    def la_bufs(s, hp, delta, tag):
        C = 64
        return dict(
            Mh=[s.a16(5 * C, f"Mh{tag}{h}", 128) for h in range(hp)],
            tok=s.a16(3 * 128, f"tok{tag}"),
            Vp=[s.a16(128, f"Vp{tag}{h}") for h in range(hp)],
            Up=[s.a16(128, f"Up{tag}{h}") for h in range(hp)],
            R_f=s.a32(128, f"Rf{tag}"), R_b=s.a16(128, f"Rb{tag}"),
            Xs=[s.a16(4 * C, f"Xs{tag}{i}") for i in range(2)])

    def segs(s):
        c = s.c
        out = []
        nb = c.T // c.TB
        for b in range(nb):
            out.append((b * c.TB, c.TB, c.CH, 0, b == 0, b == nb - 1))
        out.append((c.T, c.TS, c.TS, 1, True, True))
        return out


class KB4(KB3):
    def setup2(s):
        c = s.c
        IN, OUT = "ExternalInput", "ExternalOutput"
        L = c.DEPTH
        s.mask3 = s.sb_raw("mask3", [64, 5, 64], BF16)
        s.onesf = s.sb_raw("onesf", [128, 128], F32)
        s.blkf = s.sb_raw("blkf", [128, 128], F32)
        s.d_mask3 = s.dram("c_mask3", [64, 5, 64], BF16, IN)
        s.d_onesf = s.dram("c_onesf", [128, 128], F32, IN)
        s.d_blkf = s.dram("c_blkf", [128, 128], F32, IN)
        s.DMA(tv(s.mask3), tv(s.d_mask3))
        s.DMA(tv(s.onesf), tv(s.d_onesf))
        s.DMA(tv(s.blkf), tv(s.d_blkf))
        s.hgrn_lb = s.dram("hgrn_lb", [L, c.HGW], F32, IN)
        s.hgrn_norm = s.dram("hgrn_norm", [L, c.HGW], F32, IN)
        s.st_hg = s.dram("st_hg", [L, c.HGH, 128, 128], F32, IN)
        s.st_rw = s.dram("st_rw", [L, c.RWH, 64, 64], F32, IN)
        s.st_sh = s.dram("st_sh", [L, c.RWC], F32, IN)
        for nm, shp in [("rwkv_mu", [L, c.RWC]), ("rwkv_w0", [L, c.RWW]), ("rwkv_w2", [L, 64, c.RWW]),
                        ("rwkv_a0", [L, c.RWW]), ("rwkv_a2", [L, 64, c.RWW]), ("rwkv_g2", [L, 160, c.RWW]),
                        ("rwkv_kk", [L, c.RWW]), ("rwkv_ka", [L, c.RWW]), ("rwkv_rk", [L, c.RWW]),
                        ("rwkv_lnx_w", [L, c.RWW]), ("rwkv_lnx_b", [L, c.RWW]),
                        ("q_norm", [L, 128]), ("k_norm", [L, 128])]:
            setattr(s, nm, s.dram(nm, shp, F32, IN))
        s.hg_o = s.dram("hg_o", [L, 2, c.HGH, 128, 128], F32, OUT)
        s.rw_o = s.dram("rw_o", [L, 2, c.RWH, 64, 64], F32, OUT)
        s.sh_o = s.dram("sh_o", [L, 2, c.RWC], F32, OUT)

    def col_load(s, dst, src_vec, ncol):
        s.DMAs(dst, V(src_vec.t, src_vec.ap.rearrange("(c p) -> p c", p=128)))

    def sigmoid_(s, t, n, parts=128):
        s.ACT(t, t, AF.Exp, scale=-1.0)
        s.TS(t, t, 1.0, ALU.add)
        s.RCP(t, t)

    def phase_hgrn(s, l):
        c = s.c
        H = c.HGH
        s.phase("hg")
        TBm = c.TB
        prm = s.a32(4 * H, "prm")
        s.col_load(tv(prm)[:, 2 * H:3 * H], tv(s.hgrn_norm)[l], H)
        if l == 0:
            s.MS(tv(prm)[:, 0:H], 0.0)
        else:
            s.col_load(tv(prm)[:, 0:H], tv(s.hgrn_lb)[0], H)
            s.col_load(tv(prm)[:, 3 * H:4 * H], tv(s.hgrn_lb)[1], H)
            s.TT(tv(prm)[:, 0:H], tv(prm)[:, 0:H], tv(prm)[:, 3 * H:4 * H], ALU.subtract)
            s.ACT(tv(prm)[:, 0:H], tv(prm)[:, 0:H], AF.Exp)
            s.TS(tv(prm)[:, 0:H], tv(prm)[:, 0:H], 1.0, ALU.add)
            s.RCP(tv(prm)[:, 0:H], tv(prm)[:, 0:H])
        s.TS(tv(prm)[:, H:2 * H], tv(prm)[:, 0:H], -1.0, ALU.mult, 1.0, ALU.add)
        tmp = [s.a32(TBm, f"t{i}") for i in range(6)]
        arrs = [{k: tv(s.a16(TBm, f"{k}{h}")) for k in ("rt", "kt", "kh", "vT")} for h in range(H)]
        yTs = [s.a32(TBm, f"yT{h}") for h in range(H)]
        gCs = [s.a32(8, f"gC{h}") for h in range(H)]
        S_f = [s.a32(128, f"Sf{h}") for h in range(H)]
        S_b = [s.a16(128, f"Sb{h}") for h in range(H)]
        bufs = [s.la_bufs(1, False, f"h{h}") for h in range(H)]
        o16 = [s.a16(TBm, f"o16{i}") for i in range(2)]
        for (g0, n, C, seq, first, last) in s.segs():
            nch = n // C
            for h in range(H):
                if first:
                    if seq == 0:
                        s.MS(tv(S_f[h]), 0.0)
                    else:
                        s.DMA(tv(S_f[h]), tv(s.st_hg)[l, h])
                    s.CP(tv(S_b[h]), tv(S_f[h]))
                t = [tv(x)[:, :n] for x in tmp]
                lb = tv(prm)[:, h:h + 1]
                oml = tv(prm)[:, H + h:H + h + 1]
                s.DMA(t[0], tv(s.pT)[c.c_hq + h * 128:c.c_hq + (h + 1) * 128, g0:g0 + n])
                s.DMA(t[1], tv(s.pT)[c.c_hf + h * 128:c.c_hf + (h + 1) * 128, g0:g0 + n])
                s.ACT(t[2], t[1], AF.Exp, scale=-1.0)
                s.TS(t[3], t[2], 1.0, ALU.add)
                s.RCP(t[3], t[3])
                s.TS(t[4], t[2], lb, ALU.mult, 1.0, ALU.add)
                s.TT(t[4], t[4], t[3], ALU.mult)
                s.ACT(t[4], t[4], AF.Ln)
                s.TT(t[2], t[2], t[3], ALU.mult)
                s.TS(t[2], t[2], oml, ALU.mult)
                s.ACT(t[3], t[0], AF.Exp, scale=-1.0)
                s.TS(t[3], t[3], 1.0, ALU.add)
                s.RCP(t[3], t[3])
                s.TT(t[0], t[0], t[3], ALU.mult)
                bT, oT = s.chunk_cumsum(tmp[4], tmp[5], n, C)
                b = tv(bT)[:, :n]
                b3 = b.re("p (c t) -> p c t", t=C)
                t3v = t[3].re("p (c t) -> p c t", t=C)
                s.ACT(t[3], b, AF.Exp)
                s.STT(arrs[h]["rt"][:, :n], t[0], 128.0 ** -0.5, t[3], ALU.mult, ALU.mult)
                s.ACT(t[3], b, AF.Exp, scale=-1.0)
                s.TT(arrs[h]["kt"][:, :n], t[2], t[3], ALU.mult)
                s.TT(t3v, V(b.t, b3.ap[:, :, C - 1:C].to_broadcast([128, nch, C])), b3, ALU.subtract)
                s.ACT(t[3], t[3], AF.Exp)
                s.TT(arrs[h]["kh"][:, :n], t[2], t[3], ALU.mult)
                s.ACT(tv(gCs[h])[:, :nch], V(b.t, b3.ap[:, :, C - 1]), AF.Exp)
                s.DMA(t[0], tv(s.pT)[c.c_hi + h * 128:c.c_hi + (h + 1) * 128, g0:g0 + n])
                s.CP(arrs[h]["vT"][:, :n], t[0])
            s.run_interleaved([s.chunk_la({k: v[:, :n] for k, v in arrs[h].items()}, tv(gCs[h]), S_f[h], S_b[h],
                                          tv(yTs[h])[:, :n], n, C, 1, False, bufs[h], tv(s.mask3)[:, 1:2, :])
                               for h in range(H)])
            for h in range(H):
                t = [tv(x)[:, :n] for x in tmp]
                y = tv(yTs[h])[:, :n]
                s.TT(t[0], y, y, ALU.mult)
                ps = s.psum()
                s.MM(tv(ps)[:, :n], tv(s.onesf), t[0])
                s.TS(t[1], tv(ps)[:, :n], 1.0 / 128, ALU.mult, 1e-6, ALU.add)
                s.ACT(t[1], t[1], AF.Ln)
                s.ACT(t[1], t[1], AF.Exp, scale=-0.5)
                s.TT(t[0], y, t[1], ALU.mult)
                s.DMA(t[2], tv(s.pT)[c.c_hg + h * 128:c.c_hg + (h + 1) * 128, g0:g0 + n])
                s.sigmoid_(t[2], n)
                ob = tv(o16[h % 2])[:, :n]
                s.STT(ob, t[0], tv(prm)[:, 2 * H + h:2 * H + h + 1], t[2], ALU.mult, ALU.mult)
                s.DMA(tv(s.mixT.sub(("hg", h, g0)))[h * 128:(h + 1) * 128, g0:g0 + n], ob)
                if last:
                    s.DMA(tv(s.hg_o)[l, seq, h], tv(S_f[h]))


def consts():
    m = np.zeros((64, 5, 64), np.float32)
    p = np.arange(64)[:, None]
    j = np.arange(64)[None, :]
    m[:, 0] = p < j
    m[:, 1] = p <= j
    m[:, 2] = p < j
    m[:, 3] = p <= j
    m[:, 4] = j < p
    blk = np.zeros((128, 128), np.float32)
    blk[:64, :64] = 1
    blk[64:, 64:] = 1
    return dict(c_identb=np.eye(128).astype(ml_dtypes.bfloat16), c_identf=np.eye(128, dtype=np.float32),
                c_mask3=m.astype(ml_dtypes.bfloat16), c_onesf=np.ones((128, 128), np.float32), c_blkf=blk)


def make_in_maps(cfg, inp, ncores=8):
    L = cfg.DEPTH
    B = inp["x_prompt"].shape[0]
    cs = consts()
    maps = []
    shared = {}
    for k in ("w_in", "w_out", "w_gate", "w_up", "w_down", "ln1", "ln2", "hgrn_lb", "hgrn_norm", "rwkv_mu", "rwkv_w0",
              "rwkv_w2", "rwkv_a0", "rwkv_a2", "rwkv_g2", "rwkv_kk", "rwkv_ka", "rwkv_lnx_w", "rwkv_lnx_b", "q_norm", "k_norm"):
        shared[k] = np.ascontiguousarray(inp[k])
    shared["rwkv_rk"] = np.ascontiguousarray(inp["rwkv_rk"]).reshape(L, cfg.RWW)
    for i in range(L):
        shared[f"cache_k{i}"] = np.ascontiguousarray(inp["cache_k"][i]).reshape(cfg.NPOOL * 128, cfg.KVW)
        shared[f"cache_v{i}"] = np.ascontiguousarray(inp["cache_v"][i]).reshape(cfg.NPOOL * 128, cfg.KVW)
        shared[f"cache_kidx{i}"] = np.ascontiguousarray(inp["cache_kidx"][i]).reshape(cfg.NPOOL * 128, 128)
    for ci in range(ncores):
        m = dict(shared)
        m.update(cs)
        m["x_in"] = np.concatenate([inp["x_prompt"][ci % B], inp["x_sample"][ci]], axis=0)
        m["st_hg"] = np.ascontiguousarray(inp["state_hgrn"][:, ci])
        m["st_rw"] = np.ascontiguousarray(inp["state_rwkv"][:, ci])
        m["st_sh"] = np.ascontiguousarray(inp["state_shift"][:, ci])
        m["ptab"] = np.ascontiguousarray(inp["page_table"][ci:ci + 1]).astype(np.int32)
        maps.append(m)
    return maps


import os
DBG = os.environ.get('RWDBG', '').split(',')


class KB5(KB4):
    def rw_xs(s, dst, tP, l, ch, nr, g0, n, seq, first, mu):
        c = s.c
        r0 = c.c_rw + ch * 128
        P = tv(tP)
        if first and seq == 0:
            s.MS(P[:nr, 0:1], 0.0)
            s.DMA(P[:nr, 1:n + 1], tv(s.pT)[r0:r0 + nr, g0:g0 + n])
        elif first:
            src = tv(s.st_sh)[l, ch * 128:ch * 128 + nr]
            s.DMAs(P[:nr, 0:1], V(src.t, src.ap.rearrange("(r o) -> r o", o=1)))
            s.DMA(P[:nr, 1:n + 1], tv(s.pT)[r0:r0 + nr, g0:g0 + n])
        else:
            s.DMA(P[:nr, 0:n + 1], tv(s.pT)[r0:r0 + nr, g0 - 1:g0 + n])
        s.TT(dst[:nr], P[:nr, 0:n], P[:nr, 1:n + 1], ALU.subtract)
        s.STT(dst[:nr], dst[:nr], mu[:nr, ch:ch + 1], P[:nr, 1:n + 1], ALU.mult, ALU.add)

    def phase_rwkv(s, l):
        c = s.c
        G = c.RWW // 128
        s.phase("rw")
        TBm = c.TB
        NF = c.RWC // 128
        mu = s.a32(NF + 1, "mu")
        s.col_load(tv(mu)[:, 0:NF], tv(s.rwkv_mu)[l, 0:NF * 128], NF)
        srcm = tv(s.rwkv_mu)[l, NF * 128:c.RWC]
        s.DMAs(tv(mu)[:32, NF:NF + 1], V(srcm.t, srcm.ap.rearrange("(r o) -> r o", o=1)))
        prm = s.a32(9 * G, "prm")
        names = ["rwkv_w0", "rwkv_a0", "rwkv_kk", "rwkv_ka", "rwkv_rk", "rwkv_lnx_w", "rwkv_lnx_b"]
        pc = {}
        for i, nm in enumerate(names):
            s.col_load(tv(prm)[:, i * G:(i + 1) * G], tv(getattr(s, nm))[l], G)
            pc[nm] = tv(prm)[:, i * G:(i + 1) * G]
        s.TS(tv(prm)[:, 7 * G:8 * G], pc["rwkv_w0"], -1.0, ALU.mult)
        s.TS(tv(prm)[:, 8 * G:9 * G], pc["rwkv_a0"], -1.0, ALU.mult)
        negw0 = tv(prm)[:, 7 * G:8 * G]
        nega0 = tv(prm)[:, 8 * G:9 * G]
        w2a2 = s.a16(c.RWW, "w2a2")
        g2A = s.a16(c.RWW, "g2A")
        g2B = s.a16(c.RWW, "g2B")
        s.DMA(tv(w2a2)[0:64, :], tv(s.rwkv_w2)[l], q="pool")
        s.DMA(tv(w2a2)[64:128, :], tv(s.rwkv_a2)[l], q="pool")
        s.DMA(tv(g2A), tv(s.rwkv_g2)[l, 0:128, :], q="pool")
        s.DMA(tv(g2B)[0:32, :], tv(s.rwkv_g2)[l, 128:160, :], q="pool")
        lin = s.a16(TBm, "lin")
        sgA = s.a16(TBm, "sgA")
        sgB = s.a16(TBm, "sgB")
        tmp = [s.a32(TBm + 8, f"t{i}") for i in range(11)]
        keys = ("rt", "kt", "kh", "vT", "at", "bt", "bh")
        arrs = [{k: tv(s.a16(TBm, f"{k}{j}")) for k in keys} for j in range(G)]
        yTs = [s.a32(TBm, f"yT{j}") for j in range(G)]
        bon = [s.a32(TBm, f"bon{j}") for j in range(G)]
        gg = [s.a32(TBm, f"gg{j}") for j in range(G)]
        gCs = [s.a32(8, f"gC{j}") for j in range(G)]
        S_f = [s.a32(64, f"Sf{j}") for j in range(G)]
        S_b = [s.a16(128, f"Sb{j}") for j in range(G)]
        bufs = [s.la_bufs(2, True, f"r{j}") for j in range(G)]
        o16 = [s.a16(TBm, f"o16{i}") for i in range(2)]
        for j in range(G):
            for h in range(2):
                s.MS(tv(bufs[j]["Vp"][h]), 0.0)
                s.MS(tv(bufs[j]["Up"][h]), 0.0)
        for (g0, n, C, seq, first, last) in s.segs():
            nch = n // C
            t = [tv(x)[:, :n] for x in tmp]
            tP = tmp[10]
            s.rw_xs(t[0], tP, l, 3 * G, 128, g0, n, seq, first, tv(mu))
            s.ACT(t[1][0:64], t[0][0:64], AF.Exp, scale=-2.0)
            s.TS(t[1][0:64], t[1][0:64], 1.0, ALU.add)
            s.RCP(t[1][0:64], t[1][0:64])
            s.TS(tv(lin)[0:64, :n], t[1][0:64], 2.0, ALU.mult, -1.0, ALU.add)
            s.CP(tv(lin)[64:128, :n], t[0][64:128])
            s.rw_xs(t[0], tP, l, 3 * G + 1, 128, g0, n, seq, first, tv(mu))
            s.sigmoid_(t[0], n)
            s.CP(tv(sgA)[:, :n], t[0])
            s.rw_xs(t[0], tP, l, 3 * G + 2, 32, g0, n, seq, first, tv(mu))
            s.sigmoid_(t[0][0:32], n)
            s.CP(tv(sgB)[0:32, :n], t[0][0:32])
            for j in range(G):
                js = slice(j * 128, (j + 1) * 128)
                if first:
                    s.MS(tv(S_b[j]), 0.0)
                    if seq == 0 or "st" in DBG:
                        s.MS(tv(S_f[j]), 0.0)
                    else:
                        src = tv(s.st_rw)[l, 2 * j:2 * j + 2]
                        s.DMA(tv(tmp[9])[0:64, 0:128].re("v (h k) -> v h k", h=2), V(src.t, src.ap.rearrange("h v k -> v h k")))
                        pt = s.psum()
                        s.TR(tv(pt)[:, 0:64], tv(tmp[9])[0:64, 0:128], tv(s.identf)[0:64, 0:64])
                        s.CP(tv(S_f[j]), tv(pt)[:, 0:64], eng="dve")
                        for h in range(2):
                            hs = slice(h * 64, (h + 1) * 64)
                            s.CP(tv(S_b[j])[hs, hs], tv(S_f[j])[hs, :], eng="act")
                tr, tk, tvv, tlw, ta, tkk, tb = t[0], t[1], t[2], t[3], t[4], t[5], t[6]
                s.rw_xs(tr, tP, l, j, 128, g0, n, seq, first, tv(mu))
                s.rw_xs(tk, tP, l, G + j, 128, g0, n, seq, first, tv(mu))
                s.rw_xs(tvv, tP, l, 2 * G + j, 128, g0, n, seq, first, tv(mu))
                pw = s.psum()
                s.MM(tv(pw)[:, :n], tv(w2a2)[0:64, js], tv(lin)[0:64, :n])
                pa = s.psum()
                s.MM(tv(pa)[:, :n], tv(w2a2)[64:128, js], tv(lin)[64:128, :n])
                pg = s.psum()
                s.MM(tv(pg)[:, :n], tv(g2A)[:, js], tv(sgA)[:, :n], start=True, stop=False)
                s.MM(tv(pg)[:, :n], tv(g2B)[0:32, js], tv(sgB)[0:32, :n], start=False, stop=True)
                s.ACT(tlw, tv(pw)[:, :n], AF.Exp, bias=negw0[:, j:j + 1], scale=-1.0)
                s.TS(tlw, tlw, 1.0, ALU.add)
                s.RCP(tlw, tlw)
                s.TS(tlw, tlw, -0.6065306597126334, ALU.mult)
                s.ACT(ta, tv(pa)[:, :n], AF.Exp, bias=nega0[:, j:j + 1], scale=-1.0)
                s.TS(ta, ta, 1.0, ALU.add)
                s.RCP(ta, ta)
                s.CP(tv(gg[j])[:, :n], tv(pg)[:, :n])
                s.TS(tkk, tk, pc["rwkv_kk"][:, j:j + 1], ALU.mult)
                s.TT(t[7], tkk, tkk, ALU.mult)
                ps = s.psum()
                s.MM(tv(ps)[:, :n], tv(s.blkf), t[7])
                s.TS(t[7], tv(ps)[:, :n], 1e-24, ALU.max)
                s.ACT(t[7], t[7], AF.Ln)
                s.ACT(t[7], t[7], AF.Exp, scale=-0.5)
                s.TT(tkk, tkk, t[7], ALU.mult)
                s.TS(t[7], ta, -1.0, ALU.add, pc["rwkv_ka"][:, j:j + 1], ALU.mult)
                s.TS(t[7], t[7], 1.0, ALU.add)
                s.TT(tk, tk, t[7], ALU.mult)
                s.TT(tb, tkk, ta, ALU.mult)
                s.STT(t[7], tr, pc["rwkv_rk"][:, j:j + 1], tk, ALU.mult, ALU.mult)
                ps = s.psum()
                s.MM(tv(ps)[:, :n], tv(s.blkf), t[7])
                s.TT(tv(bon[j])[:, :n], tv(ps)[:, :n], tvv, ALU.mult)
                s.CP(t[7], tlw, eng="dve")
                bT, oT = s.chunk_cumsum(tmp[7], tmp[8], n, C)
                b = tv(bT)[:, :n]
                e = tv(oT)[:, :n]
                b3 = b.re("p (c t) -> p c t", t=C)
                e3 = e.re("p (c t) -> p c t", t=C)
                A = arrs[j]
                s.ACT(e, b, AF.Exp)
                s.TT(A["rt"][:, :n], tr, e, ALU.mult)
                s.ACT(e, b, AF.Exp, scale=-1.0)
                s.TT(A["kt"][:, :n], tk, e, ALU.mult)
                s.TT(A["bt"][:, :n], tb, e, ALU.mult)
                s.TT(e, b, tlw, ALU.subtract)
                s.ACT(e, e, AF.Exp)
                s.STT(A["at"][:, :n], tkk, -1.0, e, ALU.mult, ALU.mult)
                s.TT(e3, V(b.t, b3.ap[:, :, C - 1:C].to_broadcast([128, nch, C])), b3, ALU.subtract)
                s.ACT(e, e, AF.Exp)
                s.TT(A["kh"][:, :n], tk, e, ALU.mult)
                s.TT(A["bh"][:, :n], tb, e, ALU.mult)
                s.ACT(tv(gCs[j])[:, :nch], V(b.t, b3.ap[:, :, C - 1]), AF.Exp)
                s.CP(A["vT"][:, :n], tvv)
            if "la" not in DBG:
                s.run_interleaved([s.chunk_la({k: v[:, :n] for k, v in arrs[j].items()}, tv(gCs[j]), S_f[j], S_b[j],
                                              tv(yTs[j])[:, :n], n, C, 2, True, bufs[j], tv(s.mask3))
                                   for j in range(G)])
            for j in range(G):
                y = tv(yTs[j])[:, :n]
                ps = s.psum()
                s.MM(tv(ps)[:, :n], tv(s.blkf), y)
                s.STT(t[0], tv(ps)[:, :n], -1.0 / 64, y, ALU.mult, ALU.add)
                s.TT(t[1], t[0], t[0], ALU.mult)
                ps2 = s.psum()
                s.MM(tv(ps2)[:, :n], tv(s.blkf), t[1])
                s.TS(t[1], tv(ps2)[:, :n], 1.0 / 64, ALU.mult, 64e-5, ALU.add)
                s.ACT(t[1], t[1], AF.Ln)
                s.ACT(t[1], t[1], AF.Exp, scale=-0.5)
                s.TT(t[0], t[0], t[1], ALU.mult)
                s.TS(t[0], t[0], pc["rwkv_lnx_w"][:, j:j + 1], ALU.mult, pc["rwkv_lnx_b"][:, j:j + 1], ALU.add)
                s.TT(t[0], t[0], tv(bon[j])[:, :n], ALU.add)
                ob = tv(o16[j % 2])[:, :n]
                s.TT(ob, t[0], tv(gg[j])[:, :n], ALU.mult)
                r0 = c.HGW + j * 128
                s.DMA(tv(s.mixT.sub(("rw", j, g0)))[r0:r0 + 128, g0:g0 + n], ob)
                if last and "st" not in DBG:
                    pt = s.psum()
                    s.TR(tv(pt)[0:64, 0:128], tv(S_f[j]), tv(s.identf))
                    s.CP(tv(tmp[9])[0:64, 0:128], tv(pt)[0:64, 0:128], eng="dve")
                    dst = tv(s.rw_o)[l, seq, 2 * j:2 * j + 2]
                    s.DMA(V(dst.t, dst.ap.rearrange("h v k -> v h k")), tv(tmp[9])[0:64, 0:128].re("v (h k) -> v h k", h=2))
            if last and "sh" not in DBG:
                gl = g0 + n - 1
                dst = tv(s.sh_o)[l, seq]
                s.DMAs(V(dst.t, dst.ap.rearrange("(r o) -> r o", o=1)), tv(s.pT)[c.c_rw:c.c_rw + c.RWC, gl:gl + 1])


BIGSEL = 1.0e6


class KB6(KB5):
    def setup3(s):
        c = s.c
        IN, OUT = "ExternalInput", "ExternalOutput"
        L = c.DEPTH
        c.NCH = c.PAST // 512
        c.HT = c.IDXH * c.TS
        c.TC = c.TS * c.NCH
        c.GQ = c.ATH // c.KVH
        c.GT = c.GQ * c.TS
        assert c.TC <= 128 and c.PAST % 512 == 0
        s.cache_k = [s.dram(f"cache_k{i}", [c.NPOOL * 128, c.KVW], F32, IN) for i in range(L)]
        s.cache_v = [s.dram(f"cache_v{i}", [c.NPOOL * 128, c.KVW], F32, IN) for i in range(L)]
        s.cache_kidx = [s.dram(f"cache_kidx{i}", [c.NPOOL * 128, 128], F32, IN) for i in range(L)]
        s.ptab = s.dram("ptab", [1, c.NPAGES], I32, IN)
        s.k_o = s.dram("k_o", [L, c.NT, c.KVW], F32, OUT)
        s.v_o = s.dram("v_o", [L, c.NT, c.KVW], F32, OUT)
        s.ki_o = s.dram("ki_o", [L, c.NT, 128], F32, OUT)
        s.d_negdiag = s.dram("c_negdiag", [128, 128], F32, IN)
        s.d_iota = s.dram("c_iota", [128, 16], F32, IN)
        s.d_wsel = s.dram("c_wsel", [c.HT, c.TC + c.NCH], F32, IN)
        s.d_negnew = s.dram("c_negnew", [128, 16], F32, IN)
        s.d_blkc = s.dram("c_blkc", [128, 128], F32, IN)
        s.d_onesb = s.dram("c_onesb", [128, 128], BF16, IN)
        s.negdiag = s.sb_raw("negdiag", [128, 128], F32)
        s.iota = s.sb_raw("iota", [128, 16], F32)
        s.negnew = s.sb_raw("negnew", [128, 16], F32)
        s.blkc = s.sb_raw("blkc", [128, 128], F32)
        s.onesb = s.sb_raw("onesb", [128, 128], BF16)
        s.pti = s.sb_raw("pti", [128, c.NPAGES], I32)
        s.ptf = s.sb_raw("ptf", [128, c.NPAGES], F32)
        s.idxi = s.sb_raw("idxi", [128, c.NPAGES], I32)
        s.DMA(tv(s.negdiag), tv(s.d_negdiag))
        s.DMA(tv(s.iota), tv(s.d_iota))
        s.DMA(tv(s.negnew), tv(s.d_negnew))
        s.DMA(tv(s.blkc), tv(s.d_blkc))
        s.DMA(tv(s.onesb), tv(s.d_onesb))
        s.DMA(tv(s.pti), V(s.ptab, s.ptab.h[0].partition_broadcast(128)))
        s.CP(tv(s.ptf), tv(s.pti), eng="dve")
        s.TS(tv(s.ptf), tv(s.ptf), 128.0, ALU.mult, tv(s.iota)[:, 0:1], ALU.add)
        s.CP(tv(s.idxi), tv(s.ptf), eng="dve")

    def GATHER(s, out, table, idx):
        def f(e):
            return e.indirect_dma_start(out=out.ap, out_offset=None, in_=table.ap,
                                        in_offset=bass.IndirectOffsetOnAxis(ap=idx.ap, axis=0))
        s.op("pool", f, rd=[table, idx], wr=[out], dma=True)

    def bisect(s, acc, np_, S, K, sc, junk, blk=None, R=1024.0, iters=36):
        lo, mid, cnt, lB = sc[:np_, 0:1], sc[:np_, 1:2], sc[:np_, 2:3], sc[:np_, 3:4]
        s.MS(lo, -R)
        step = R
        for it in range(iters):
            s.TS(mid, lo, step, ALU.add)
            s.TS(junk[:np_, :S], acc, mid, ALU.is_ge, 0.0, ALU.add, accum=cnt)
            if blk is not None:
                ps = s.psum()
                s.MM(tv(ps)[:np_, 0:1], blk[:np_, :np_], cnt)
                cv = tv(ps)[:np_, 0:1]
            else:
                cv = cnt
            s.TS(lB, cv, K - 0.5, ALU.is_lt, -BIGSEL, ALU.mult)
            s.STT(lo, mid, lB, lo, ALU.add, ALU.max)
            step *= 0.5
        return lo

    def q_prep(s, g0, nq, qraw, qsq, rs, qTn, gq):
        c = s.c
        W = c.ATH * nq
        src = tv(s.pT)[c.c_aq:c.c_aq + c.ATW, g0:g0 + nq]
        s.DMA(qraw[:, :W].re("p (h q) -> p h q", q=nq), V(src.t, src.ap.rearrange("(h d) q -> d h q", d=128)))
        s.TT(qsq[:, :W], qraw[:, :W], qraw[:, :W], ALU.mult)
        for c0 in range(0, W, 512):
            w = min(512, W - c0)
            ps = s.psum()
            s.MM(tv(ps)[:, :w], tv(s.onesf), qsq[:, c0:c0 + w])
            s.TS(rs[:, :w], tv(ps)[:, :w], 1.0 / 128, ALU.mult, 1e-6, ALU.add)
            s.ACT(rs[:, :w], rs[:, :w], AF.Ln)
            s.ACT(rs[:, :w], rs[:, :w], AF.Exp, scale=-0.5)
            s.STT(qTn[:, c0:c0 + w], qraw[:, c0:c0 + w], gq, rs[:, :w], ALU.mult, ALU.mult)

    def phase_dsa(s, l):
        c = s.c
        s.phase("dsa")
        psf_all = s.psf
        s.psf = psf_all[:4]
        s.pfi = -1
        accO, accS = psf_all[4], psf_all[5]
        try:
            s._dsa(l, accO, accS)
        finally:
            s.psf = psf_all
            s.pfi = -1

    def _dsa(s, l, accO, accS):
        c = s.c
        NT, T, TS, KV, IH = c.NT, c.T, c.TS, c.KVH, c.IDXH
        tiles = s.tiles128([(0, NT, 0)])
        ntile = len(tiles)
        nqb = T // 128
        K_p = min(c.TOPK, T // 4)
        K_s = min(c.TOPK, (c.PAST + TS) // 4)
        scale = 128.0 ** -0.5
        r0mix = c.HGW + c.RWW
        NTp = (NT + 15) // 16 * 16
        kTn = tv(s.a16(KV * NTp, "kTn")).re("p (n t) -> p n t", t=NTp)
        kiT = tv(s.a16(NTp, "kiT"))
        Vtok = tv(s.a16(ntile * KV * 128, "Vtok")).re("p (a n d) -> p a n d", n=KV, d=128)
        gq = tv(s.a32(1, "gq"))[:, 0:1]
        gk = tv(s.a32(1, "gk"))[:, 0:1]
        wir = tv(s.a32(NTp, "wir"))
        wiT = tv(s.a32(ntile * 16, "wiT")).re("p (a h) -> p a h", h=16)
        s.col_load(gq, tv(s.q_norm)[l], 1)
        s.col_load(gk, tv(s.k_norm)[l], 1)
        raw = [tv(s.a32(512, f"raw{i}")) for i in range(2)]
        sq = tv(s.a32(512, "sq"))
        rs = tv(s.a32(512, "rs"))
        kn = tv(s.a32(512, "kn"))
        stg = [tv(s.a32(128, f"stg{i}")) for i in range(3)]
        ri = [0]
        si = [0]

        def nraw():
            ri[0] += 1
            return raw[ri[0] % 2]

        def nstg():
            si[0] += 1
            return stg[si[0] % 3]
        chunks = [(g, min(512, NT - g)) for g in range(0, NT, 512)]
        for n in range(KV):
            for (g0, nn) in chunks:
                kr = nraw()
                r0 = c.c_ak + n * 128
                s.DMA(kr[:, :nn], tv(s.pT)[r0:r0 + 128, g0:g0 + nn])
                s.TT(sq[:, :nn], kr[:, :nn], kr[:, :nn], ALU.mult)
                ps = s.psum()
                s.MM(tv(ps)[:, :nn], tv(s.onesf), sq[:, :nn])
                s.TS(rs[:, :nn], tv(ps)[:, :nn], 1.0 / 128, ALU.mult, 1e-6, ALU.add)
                s.ACT(rs[:, :nn], rs[:, :nn], AF.Ln)
                s.ACT(rs[:, :nn], rs[:, :nn], AF.Exp, scale=-0.5)
                s.STT(kn[:, :nn], kr[:, :nn], gk, rs[:, :nn], ALU.mult, ALU.mult)
                s.CP(kTn[:, n, g0:g0 + nn], kn[:, :nn], eng="act")
                for o in range(0, nn, 128):
                    m = min(128, nn - o)
                    pt = s.psum()
                    s.TR(tv(pt)[:m, 0:128], kn[:, o:o + m], tv(s.identf))
                    st_ = nstg()
                    s.CP(st_[:m, :], tv(pt)[:m, 0:128])
                    s.DMA(tv(s.k_o.sub((l, n, g0 + o)))[l, g0 + o:g0 + o + m, n * 128:(n + 1) * 128], st_[:m, :])
        for n in range(KV):
            for (g0, nn) in chunks:
                vr = nraw()
                r0 = c.c_av + n * 128
                s.DMA(vr[:, :nn], tv(s.pT)[r0:r0 + 128, g0:g0 + nn])
                for o in range(0, nn, 128):
                    m = min(128, nn - o)
                    a = (g0 + o) // 128
                    pt = s.psum()
                    s.TR(tv(pt)[:m, 0:128], vr[:, o:o + m], tv(s.identf))
                    st_ = nstg()
                    s.CP(st_[:m, :], tv(pt)[:m, 0:128], eng="act")
                    s.CP(Vtok[:m, a, n, :], tv(pt)[:m, 0:128], eng="dve")
                    s.DMA(tv(s.v_o.sub((l, n, g0 + o)))[l, g0 + o:g0 + o + m, n * 128:(n + 1) * 128], st_[:m, :])
        for (g0, nn) in chunks:
            kr = nraw()
            s.DMA(kr[:, :nn], tv(s.pT)[c.c_aki:c.c_aki + 128, g0:g0 + nn])
            s.CP(kiT[:, g0:g0 + nn], kr[:, :nn], eng="act")
            for o in range(0, nn, 128):
                m = min(128, nn - o)
                pt = s.psum()
                s.TR(tv(pt)[:m, 0:128], kr[:, o:o + m], tv(s.identf))
                st_ = nstg()
                s.CP(st_[:m, :], tv(pt)[:m, 0:128])
                s.DMA(tv(s.ki_o.sub((l, g0 + o)))[l, g0 + o:g0 + o + m, :], st_[:m, :])
        wscale = float(c.IDXH ** -0.5 * 128.0 ** -0.5)
        s.DMA(wir[:IH, :NT], tv(s.pT)[c.c_awi:c.c_awi + IH, 0:NT])
        for (col, m, g0) in tiles[:nqb]:
            a = g0 // 128
            pt = s.psum()
            s.TR(tv(pt)[:m, 0:IH], wir[:IH, g0:g0 + m], tv(s.identf)[:IH, :IH])
            s.TS(wiT[:m, a, 0:IH], tv(pt)[:m, 0:IH], wscale, ALU.mult)

        Smax = T
        acc = tv(s.a32(Smax, "acc"))
        junk = tv(s.a16(max(Smax, 528), "junk"))
        maskb = tv(s.a16(max(Smax, 528), "maskb"))
        maskT = tv(s.a16(nqb * 128, "maskT")).re("p (a q) -> p a q", q=128)
        qir = tv(s.a32(IH * 128, "qir"))
        qib = tv(s.a16(IH * 128, "qib")).re("p (h q) -> p h q", q=128)
        rl = [tv(s.a32(512, f"rl{i}")) for i in range(2)]
        qraw = tv(s.a32(c.ATH * 128, "qraw"))
        qsq = tv(s.a32(c.ATH * 128, "qsq"))
        qTn = tv(s.a16(c.ATH * 128, "qTn"))
        Eb = [tv(s.a16(512, f"E{i}")) for i in range(2)]
        PTb = [tv(s.a16(512, f"PT{i}")) for i in range(2)]
        rsum = tv(s.a32(512, "rsum"))
        o16 = [tv(s.a16(512, f"o16{i}")) for i in range(2)]
        sc = tv(s.a32(16, "bsc"))
        GQ = c.GQ
        ei = 0
        for qb in range(nqb):
            g0 = qb * 128
            S = g0 + 128
            src = tv(s.pT)[c.c_aqi:c.c_aqi + IH * 128, g0:g0 + 128]
            s.DMA(qir.re("p (h q) -> p h q", q=128), V(src.t, src.ap.rearrange("(h d) q -> d h q", d=128)))
            s.CP(qib, qir.re("p (h q) -> p h q", q=128), eng="act")
            for k0 in range(0, S, 512):
                w = min(512, S - k0)
                for h in range(IH):
                    ps = s.psum()
                    s.MM(tv(ps)[:, :w], qib[:, h, :], kiT[:, k0:k0 + w])
                    if h == 0:
                        s.TS(acc[:, k0:k0 + w], tv(ps)[:, :w], 0.0, ALU.max, wiT[:, qb, 0:1], ALU.mult)
                    else:
                        r = rl[h % 2]
                        s.ACT(r[:, :w], tv(ps)[:, :w], AF.Relu)
                        s.STT(acc[:, k0:k0 + w], r[:, :w], wiT[:, qb, h:h + 1], acc[:, k0:k0 + w], ALU.mult, ALU.add)
            s.TT(acc[:, S - 128:S], acc[:, S - 128:S], tv(s.negdiag), ALU.add)
            lo = s.bisect(acc[:, :S], 128, S, K_p, sc, junk)
            s.TS(maskb[:, :S], acc[:, :S], lo, ALU.is_ge)
            if getattr(s, "dbg_dsa", False):
                d1 = s.dram(f"dbg_acc{l}_{qb}", [128, S], F32, "ExternalOutput")
                d2 = s.dram(f"dbg_sc{l}_{qb}", [128, 4], F32, "ExternalOutput")
                s.DMA(tv(d1), acc[:, :S])
                s.DMA(tv(d2), sc[:, 0:4])
            for kt0 in range(0, qb + 1, 8):
                kn_ = min(8, qb + 1 - kt0)
                pb = s.psum(bf=True)
                for j in range(kn_):
                    s.TR(tv(pb)[:, j * 128:(j + 1) * 128], maskb[:, (kt0 + j) * 128:(kt0 + j + 1) * 128], tv(s.identb))
                s.CP(maskT[:, kt0:kt0 + kn_, :], tv(pb)[:, 0:kn_ * 128].re("p (a q) -> p a q", q=128))
            s.q_prep(g0, 128, qraw, qsq, rs, qTn, gq)
            qv = qTn.re("p (h q) -> p h q", q=128)
            for n in range(KV):
                for kt in range(qb + 1):
                    ei += 1
                    E, PT = Eb[ei % 2], PTb[ei % 2]
                    pS = s.psum()
                    s.MM(tv(pS)[:, :GQ * 128], kTn[:, n, kt * 128:(kt + 1) * 128], qTn[:, n * GQ * 128:(n + 1) * GQ * 128])
                    s.ACT(E[:, :GQ * 128], tv(pS)[:, :GQ * 128], AF.Exp, scale=scale)
                    mv = maskT[:, kt, :]
                    s.TT(PT[:, :GQ * 128].re("p (g q) -> p g q", q=128), E[:, :GQ * 128].re("p (g q) -> p g q", q=128),
                         V(mv.t, mv.ap.unsqueeze(1).to_broadcast([128, GQ, 128])), ALU.mult)
                    s.MM(tv(accO)[:, :GQ * 128], Vtok[:, kt, n, :], PT[:, :GQ * 128], start=(kt == 0), stop=(kt == qb))
                    s.MM(tv(accS)[:, :GQ * 128], tv(s.onesb), PT[:, :GQ * 128], start=(kt == 0), stop=(kt == qb))
                s.RCP(rsum[:, :GQ * 128], tv(accS)[:, :GQ * 128])
                ob = o16[n % 2]
                s.TT(ob[:, :GQ * 128], tv(accO)[:, :GQ * 128], rsum[:, :GQ * 128], ALU.mult)
                r0 = r0mix + n * GQ * 128
                dst = tv(s.mixT.sub(("at", n, g0)))[r0:r0 + GQ * 128, g0:g0 + 128]
                s.DMA(V(dst.t, dst.ap.rearrange("(g d) q -> d g q", d=128)), ob[:, :GQ * 128].re("p (g q) -> p g q", q=128))

        HT, TC, NCH, GT = c.HT, c.TC, c.NCH, c.GT
        PAST, NPG = c.PAST, c.NPAGES
        kiTp = tv(s.a16(PAST, "kiTp"))
        kipg = [tv(s.a32(128, f"kipg{i}")) for i in range(4)]
        for pg0 in range(0, NPG, 4):
            pt = s.psum()
            for j in range(4):
                pg = pg0 + j
                kp = kipg[pg % 4]
                s.GATHER(kp, tv(s.cache_kidx[l]), tv(s.idxi)[:, pg:pg + 1])
                s.TR(tv(pt)[:, j * 128:(j + 1) * 128], kp, tv(s.identf))
            s.CP(kiTp[:, pg0 * 128:(pg0 + 4) * 128], tv(pt)[:, 0:512])
        qisr = tv(s.a32(HT, "qisr"))
        qis = tv(s.a16(HT, "qis"))
        src = tv(s.pT)[c.c_aqi:c.c_aqi + IH * 128, T:T + TS]
        s.DMAs(qisr[:, :HT].re("p (h t) -> p h t", t=TS), V(src.t, src.ap.rearrange("(h d) t -> d h t", d=128)))
        s.CP(qis[:, :HT], qisr[:, :HT], eng="dve")
        wcol = tv(s.a32(1, "wcol"))
        for h in range(IH):
            srcw = tv(s.pT)[c.c_awi + h, T:T + TS]
            s.DMAs(wcol[h * TS:(h + 1) * TS, 0:1], V(srcw.t, srcw.ap.rearrange("(t o) -> t o", o=1)))
        wsel = tv(s.a32(TC + NCH, "wsel"))
        s.DMA(wsel[:HT, :TC + NCH], tv(s.d_wsel))
        s.TS(wsel[:HT, :TC + NCH], wsel[:HT, :TC + NCH], wcol[:HT, 0:1], ALU.mult)

        wtmp = [tv(s.a32(TC, f"wtmp{i}")) for i in range(2)]

        def wsv_(cch):
            w_ = wtmp[cch % 2]
            s.CP(w_[:HT, :TC], wsel[:HT, NCH - 1 - cch:NCH - 1 - cch + TC], eng="dve")
            return w_[:HT, :TC]
        Rr = [tv(s.a32(512, f"Rr{i}")) for i in range(2)]
        for cch in range(NCH):
            ps = s.psum()
            s.MM(tv(ps)[:HT, :512], qis[:, :HT], kiTp[:, cch * 512:(cch + 1) * 512])
            R_ = Rr[cch % 2]
            s.ACT(R_[:HT, :512], tv(ps)[:HT, :512], AF.Relu)
            s.MM(tv(accO)[:TC, :512], wsv_(cch), R_[:HT, :512], start=(cch == 0), stop=(cch == NCH - 1))
        ps = s.psum()
        s.MM(tv(ps)[:HT, :TS], qis[:, :HT], kiT[:, T:T + TS])
        Rn = tv(s.a32(16, "Rn"))
        s.ACT(Rn[:HT, :TS], tv(ps)[:HT, :TS], AF.Relu)
        s.MM(tv(accS)[:TC, :TS], wsv_(0), Rn[:HT, :TS])
        acc_s = tv(s.a32(528, "acc_s"))
        SS = 512 + TS
        s.CP(acc_s[:TC, 0:512], tv(accO)[:TC, 0:512], eng="dve")
        s.TT(acc_s[:TC, 512:SS], tv(accS)[:TC, 0:TS], tv(s.negnew)[:TC, 0:TS], ALU.add)
        lo = s.bisect(acc_s[:TC, :SS], TC, SS, K_s, sc, junk, blk=tv(s.blkc))
        s.TS(maskb[:TC, :SS], acc_s[:TC, :SS], lo, ALU.is_ge)
        maskTs = tv(s.a16(4 * 128, "maskTs")).re("p (j m) -> p j m", m=128)
        maskTn = tv(s.a16(128, "maskTn"))
        pb = s.psum(bf=True)
        for j in range(4):
            s.TR(tv(pb)[:, j * 128:j * 128 + TC], maskb[:TC, j * 128:(j + 1) * 128], tv(s.identb)[:TC, :TC])
        s.CP(maskTs[:, :, :TC], tv(pb)[:, 0:512].re("p (j m) -> p j m", m=128)[:, :, :TC], eng="dve")
        pb = s.psum(bf=True)
        s.TR(tv(pb)[:TS, 0:TC], maskb[:TC, 512:SS], tv(s.identb)[:TC, :TC])
        s.CP(maskTn[:TS, :TC], tv(pb)[:TS, 0:TC], eng="dve")
        qTs = tv(s.a16(c.ATH * TS, "qTs"))
        s.q_prep(T, TS, qraw, qsq, rs, qTs, gq)
        NG = KV * GT
        kpg = [tv(s.a32(c.KVW, f"kpg{i}")) for i in range(3)]
        vpg = [tv(s.a32(c.KVW, f"vpg{i}")) for i in range(3)]
        kTpg = [tv(s.a16(KV * 128, f"kTpg{i}")) for i in range(3)]
        Vp = [tv(s.a16(c.KVW, f"Vp{i}")) for i in range(3)]
        Es = [tv(s.a16(max(NG, 16), f"Es{i}")) for i in range(2)]
        PTs = [tv(s.a16(max(NG, 16), f"PTs{i}")) for i in range(2)]
        accOs = tv(s.a32(2 * NG, "accOs"))

        def attend(np_, kT_of, v_of, mview, first, i):
            pS = s.psum()
            for n in range(KV):
                s.MM(tv(pS)[:np_, n * GT:(n + 1) * GT], kT_of(n), qTs[:, n * GT:(n + 1) * GT])
            E, PT = Es[i % 2], PTs[i % 2]
            s.ACT(E[:np_, :NG], tv(pS)[:np_, :NG], AF.Exp, scale=scale)
            s.TT(PT[:np_, :NG].re("p (a t) -> p a t", t=TS), E[:np_, :NG].re("p (a t) -> p a t", t=TS),
                 V(mview.t, mview.ap.unsqueeze(1).to_broadcast([np_, KV * GQ, TS])), ALU.mult)
            pO = s.psum()
            for n in range(KV):
                s.MM(tv(pO)[:, n * GT:(n + 1) * GT], v_of(n), PT[:np_, n * GT:(n + 1) * GT])
            s.MM(tv(pO)[:, NG:2 * NG], tv(s.onesb)[:np_, :], PT[:np_, :NG])
            if first:
                s.CP(accOs[:, :2 * NG], tv(pO)[:, :2 * NG], eng="dve")
            else:
                s.TT(accOs[:, :2 * NG], accOs[:, :2 * NG], tv(pO)[:, :2 * NG], ALU.add)

        for pg in range(NPG):
            cch, j = pg // 4, pg % 4
            kp, vp = kpg[pg % 3], vpg[pg % 3]
            s.GATHER(kp, tv(s.cache_k[l]), tv(s.idxi)[:, pg:pg + 1])
            s.GATHER(vp, tv(s.cache_v[l]), tv(s.idxi)[:, pg:pg + 1])
            pt = s.psum()
            for n in range(KV):
                s.TR(tv(pt)[:, n * 128:(n + 1) * 128], kp[:, n * 128:(n + 1) * 128], tv(s.identf))
            kT_ = kTpg[pg % 3]
            s.CP(kT_[:, :KV * 128], tv(pt)[:, :KV * 128], eng="act")
            V_ = Vp[pg % 3]
            s.CP(V_[:, :c.KVW], vp[:, :c.KVW], eng="dve")
            mv = maskTs[:, j, :TC].re("p (t c) -> p t c", c=NCH)[:, :, cch]
            attend(128, lambda n: kT_[:, n * 128:(n + 1) * 128], lambda n: V_[:, n * 128:(n + 1) * 128], mv, pg == 0, pg)
        aS = (T // 128)
        mvn = maskTn[:TS, :TC].re("p (t c) -> p t c", c=NCH)[:, :, 0]
        attend(TS, lambda n: kTn[:, n, T:T + TS], lambda n: Vtok[:TS, aS, n, :], mvn, False, NPG)
        rss = tv(s.a32(NG, "rss"))
        os16 = tv(s.a16(max(NG, 16), "os16"))
        s.RCP(rss[:, :NG], accOs[:, NG:2 * NG])
        s.TT(os16[:, :NG], accOs[:, :NG], rss[:, :NG], ALU.mult)
        dst = tv(s.mixT.sub(("at", "s")))[r0mix:r0mix + c.ATW, T:T + TS]
        s.DMAs(V(dst.t, dst.ap.rearrange("(h d) t -> d h t", d=128)), os16[:, :NG].re("p (h t) -> p h t", t=TS))


def consts2(cfg):
    c = cfg
    NCH = c.PAST // 512
    HT = c.IDXH * c.TS
    TC = c.TS * NCH
    p = np.arange(128)[:, None]
    j = np.arange(128)[None, :]
    negdiag = np.where(j <= p, 0.0, -1e30).astype(np.float32)
    iota = np.tile(np.arange(128, dtype=np.float32)[:, None], (1, 16))
    wscale = float(c.IDXH ** -0.5 * 128.0 ** -0.5)
    wsel = np.zeros((HT, TC + NCH), np.float32)
    for h in range(c.IDXH):
        for t in range(c.TS):
            wsel[h * c.TS + t, t * NCH + NCH - 1] = wscale
    negnew = np.full((128, 16), -1e30, np.float32)
    for t in range(c.TS):
        for sidx in range(c.TS):
            if sidx <= t:
                negnew[t * NCH + 0, sidx] = 0.0
    blkc = np.zeros((128, 128), np.float32)
    for t in range(c.TS):
        blkc[t * NCH:(t + 1) * NCH, t * NCH:(t + 1) * NCH] = 1.0
    return dict(c_negdiag=negdiag, c_iota=iota, c_wsel=wsel, c_negnew=negnew, c_blkc=blkc,
                c_onesb=np.ones((128, 128), np.float32).astype(ml_dtypes.bfloat16))


def build_full(cfg, layers=None):
    b = KB6(cfg)
    b.setup()
    b.setup2()
    b.setup3()
    b.DMA(tv(b.xres), tv(b.x_in))
    for l in range(cfg.DEPTH if layers is None else layers):
        b.phase_win(l)
        b.phase_hgrn(l)
        b.phase_rwkv(l)
        b.phase_dsa(l)
        b.phase_wout(l)
        b.phase_ffn(l)
    b.barrier()
    b.P.emit()
    return b


def run_full(cfg, inp, layers=None):
    b = build_full(cfg, layers)
    maps = make_in_maps(cfg, inp)
    c2 = consts2(cfg)
    for m in maps:
        m.update(c2)
    maps = [{k: np.ascontiguousarray(v).astype(np_dt(b.din[k][1])) if v.dtype != np_dt(b.din[k][1]) else np.ascontiguousarray(v)
             for k, v in m.items() if k in b.din} for m in maps]
    for k in b.din:
        assert k in maps[0], k
    res = run_bass_kernel_spmd(b.nc, maps, core_ids=list(range(8)))
    return b, res.results


def assemble(cfg, R, B=4, DB=8):
    c = cfg
    L, T, TS = c.DEPTH, c.T, c.TS
    f = np.float32
    y_p = np.stack([R[b]["xres"][:T] for b in range(B)]).astype(f)
    y_s = np.stack([R[i]["xres"][T:] for i in range(DB)]).astype(f)

    def pl(name, shp):
        return np.stack([np.stack([R[b][name][l, :T].reshape(T, *shp) for b in range(B)]) for l in range(L)]).astype(f)

    def sl(name, shp):
        return np.stack([np.stack([R[i][name][l, T:].reshape(TS, *shp) for i in range(DB)]) for l in range(L)]).astype(f)

    def st(name, seq, n):
        return np.stack([np.stack([R[i][name][l, seq] for i in range(n)]) for l in range(L)]).astype(f)
    return (y_p, y_s,
            pl("k_o", (c.KVH, 128)), pl("v_o", (c.KVH, 128)), pl("ki_o", (128,)),
            st("hg_o", 0, B), st("rw_o", 0, B), st("sh_o", 0, B),
            sl("k_o", (c.KVH, 128)), sl("v_o", (c.KVH, 128)), sl("ki_o", (128,)),
            st("hg_o", 1, DB), st("rw_o", 1, DB), st("sh_o", 1, DB))


def kernel(**inputs):
    cfg = Cfg()
    inp = {k: np.asarray(v) for k, v in inputs.items()}
    b, R = run_full(cfg, inp)
    return assemble(cfg, R)
```

```python
import numpy as np
import concourse.bass as bass
import concourse.mybir as mybir

F32 = mybir.dt.float32
BF16 = mybir.dt.bfloat16
I32 = mybir.dt.int32
ALU = mybir.AluOpType
AF = mybir.ActivationFunctionType
AX = mybir.AxisListType

COMPUTE = ("pe", "act", "dve", "pool")
NDSEM = 6


class T:
    def __init__(self, h, name):
        self.h = h
        self.name = name
        self.lw = None
        self.rd = []
        self.kids = {}
        self.parent = None
        self.excl = False

    def sub(self, key):
        k = self.kids.get(key)
        if k is None:
            k = T(self.h, f"{self.name}.{key}")
            k.parent = self
            self.kids[key] = k
        return k

    def __getitem__(self, idx):
        return self.h[idx]


class Prog:
    def __init__(self, nc):
        self.nc = nc
        self.ops = []
        self.n = 0

    def _deps(self, rd, wr, idx):
        deps = set()

        def nodes(t):
            if t.parent is not None:
                return [t], [t.parent]
            return [t] + list(t.kids.values()), []

        for t in rd:
            own, look = nodes(t)
            for n in own + look:
                if n.lw is not None:
                    deps.add(n.lw)
        for t in wr:
            own, look = nodes(t)
            for n in own + look:
                if n.lw is not None:
                    deps.add(n.lw)
                deps.update(n.rd)
        for t in rd:
            own, _ = nodes(t)
            for n in own:
                n.rd.append(idx)
        for t in wr:
            own, _ = nodes(t)
            for n in own:
                n.lw = idx
                n.rd = []
        deps.discard(idx)
        return deps

    def op(self, eng, fn, rd=(), wr=(), dma=False):
        idx = len(self.ops)
        rd = list(rd)
        wr = list(wr) + [t for t in rd if t.excl]
        rd = [t for t in rd if not t.excl]
        deps = self._deps(rd, wr, idx)
        self.ops.append(dict(eng=eng, fn=fn, deps=deps, dma=dma, inc=False))
        return idx

    def emit(self):
        nc = self.nc
        ops = self.ops
        engs = ["pe", "act", "dve", "pool", "sp"]
        for i, o in enumerate(ops):
            for d in o["deps"]:
                p = ops[d]
                if p["dma"]:
                    continue
                if p["eng"] == o["eng"] and o["eng"] == "pe":
                    continue
                p["inc"] = True
        import contextlib
        with contextlib.ExitStack() as st:
            csem = {e: st.enter_context(nc.semaphore(f"c_{e}")) for e in engs}
            dsem = {e: [st.enter_context(nc.semaphore(f"d_{e}{j}")) for j in range(NDSEM)]
                    for e in ("sp", "pool", "act")}
            ccount = {e: 0 for e in engs}
            dcount = {e: 0 for e in dsem}
            for o in ops:
                e = o["eng"]
                if o["dma"]:
                    n = dcount[e]
                    dcount[e] += 1
                    j = n % NDSEM
                    o["sem"] = dsem[e][j]
                    o["val"] = 16 * (n // NDSEM + 1)
                    o["prev"] = (dsem[e][j], 16 * (n // NDSEM)) if n >= NDSEM else None
                elif o["inc"]:
                    ccount[e] += 1
                    o["sem"] = csem[e]
                    o["val"] = ccount[e]
            waited = {e: {} for e in engs}
            for o in ops:
                e = o["eng"]
                w = {}
                for d in o["deps"]:
                    p = ops[d]
                    if (not p["dma"]) and p["eng"] == e and e == "pe":
                        continue
                    s, v = p["sem"], p["val"]
                    k = id(s)
                    if w.get(k, (None, 0))[1] < v:
                        w[k] = (s, v)
                if o["dma"] and o["prev"] is not None:
                    s, v = o["prev"]
                    k = id(s)
                    if w.get(k, (None, 0))[1] < v:
                        w[k] = (s, v)
                waits = []
                for k, (s, v) in w.items():
                    if waited[e].get(k, 0) < v:
                        waited[e][k] = v
                        waits.append((s, v))
                o["waits"] = waits
            finals = []
            for e in dsem:
                n = dcount[e]
                for j in range(min(n, NDSEM)):
                    cnt = (n - j + NDSEM - 1) // NDSEM
                    finals.append((dsem[e][j], 16 * cnt))
            block = st.enter_context(nc.Block())

            def run(eng_name, engine):
                for o in ops:
                    if o["eng"] != eng_name:
                        continue
                    for s, v in o["waits"]:
                        engine.wait_ge(s, v)
                    inst = o["fn"](engine)
                    if o["dma"]:
                        inst.then_inc(o["sem"], 16)
                    elif o["inc"]:
                        inst.then_inc(o["sem"], 1)
                if eng_name == "sp":
                    for s, v in finals:
                        engine.wait_ge(s, v)

            @block.tensor
            def _(e):
                run("pe", e)

            @block.scalar
            def _(e):
                run("act", e)

            @block.vector
            def _(e):
                run("dve", e)

            @block.gpsimd
            def _(e):
                run("pool", e)

            @block.sync
            def _(e):
                run("sp", e)
        return nc

import contextlib
import ml_dtypes
from concourse.bass_utils import run_bass_kernel_spmd


class V:
    def __init__(self, t, ap):
        self.t = t
        self.ap = ap

    def __getitem__(self, idx):
        return V(self.t, self.ap[idx])

    def re(self, pat, **kw):
        return V(self.t, self.ap.rearrange(pat, **kw))

    def bc(self, shape, axis):
        return V(self.t, self.ap.unsqueeze(axis).to_broadcast(shape))


def tv(t, idx=None):
    ap = t.h[:] if idx is None else t.h[idx]
    return V(t, ap)


class Cfg:
    def __init__(s, D=4096, T=2048, TS=4, PAST=16384, HGH=8, RWH=16, ATH=16, KVH=4, IDXH=8,
                 TOPK=256, DEPTH=2, NPOOL=1280, DFF=None):
        s.D, s.T, s.TS, s.PAST, s.HGH, s.RWH, s.ATH, s.KVH, s.IDXH = D, T, TS, PAST, HGH, RWH, ATH, KVH, IDXH
        s.TOPK, s.DEPTH, s.NPOOL = TOPK, DEPTH, NPOOL
        s.HGW, s.RWW, s.ATW, s.KVW = HGH * 128, RWH * 64, ATH * 128, KVH * 128
        assert s.HGW + s.RWW + s.ATW == D
        s.RWC = 3 * s.RWW + 64 + 64 + 160
        s.c_hq, s.c_hf, s.c_hi, s.c_hg = 0, s.HGW, 2 * s.HGW, 3 * s.HGW
        s.c_rw = 4 * s.HGW
        s.c_aq = s.c_rw + s.RWC
        s.c_ak = s.c_aq + s.ATW
        s.c_av = s.c_ak + s.KVW
        s.c_aqi = s.c_av + s.KVW
        s.c_aki = s.c_aqi + IDXH * 128
        s.c_awi = s.c_aki + 128
        s.INC = s.c_awi + IDXH
        s.DFF = DFF if DFF is not None else -(-8 * D // (3 * 256)) * 256
        s.NT = T + TS
        s.NPAGES = PAST // 128
        s.KC = D // 128
        s.FC = s.DFF // 128
        s.TB = min(512, T)
        s.CH = min(64, T)


class KB:
    def __init__(s, cfg):
        s.c = cfg
        s.nc = bass.Bass("TRN2", target_bir_lowering=False)
        s.P = Prog(s.nc)
        s.st = contextlib.ExitStack()
        s.din = {}
        s.dout = {}
        s.rr = 0

    def dram(s, name, shape, dt=F32, kind="Internal"):
        h = s.nc.dram_tensor(name, list(shape), dt, kind=kind).ap()
        t = T(h, name)
        if kind == "ExternalInput":
            s.din[name] = (tuple(shape), dt)
        if kind == "ExternalOutput":
            s.dout[name] = (tuple(shape), dt)
        return t

    def sb_raw(s, name, shape, dt):
        return T(s.st.enter_context(s.nc.sbuf_tensor(name, list(shape), dt)), name)

    def ps_raw(s, name, shape, dt):
        t = T(s.st.enter_context(s.nc.psum_tensor(name, list(shape), dt)), name)
        t.excl = True
        return t

    def phase(s, name):
        s.barrier()
        s.o32 = 0
        s.o16 = 0
        s.pname = name
        s.pcount = getattr(s, "pcount", 0) + 1

    def a32(s, n, name="t", parts=128):
        assert s.o32 + n <= s.A32, (s.pname, name, s.o32, n)
        t = T(s.arena32.h[0:parts, s.o32:s.o32 + n], f"{s.pname}{s.pcount}.{name}")
        s.o32 += (n + 15) // 16 * 16
        return t

    def a16(s, n, name="t", parts=128):
        assert s.o16 + n <= s.A16, (s.pname, name, s.o16, n)
        t = T(s.arena16.h[0:parts, s.o16:s.o16 + n], f"{s.pname}{s.pcount}.{name}")
        s.o16 += (n + 15) // 16 * 16
        return t

    def barrier(s):
        P = s.P
        idx = len(P.ops)
        deps = set(getattr(s, "_since", []))
        P.ops.append(dict(eng="dve", fn=lambda e: e.memset(s.bar.h[0:1, 0:1], 0.0), deps=deps, dma=False, inc=False))
        s._since = [idx]
        s._bar = idx

    def op(s, eng, fn, rd=(), wr=(), dma=False):
        if getattr(s, "_lim", None) is not None:
            s._cnt = getattr(s, "_cnt", 0) + 1
            if s._cnt > s._lim:
                return None
        i = s.P.op(eng, fn, [v.t if isinstance(v, V) else v for v in rd], [v.t if isinstance(v, V) else v for v in wr], dma)
        if getattr(s, "_bar", None) is not None:
            s.P.ops[i]["deps"].add(s._bar)
        s._since.append(i)
        return i

    def MM(s, out, lhsT, rhs, start=True, stop=True):
        s.op("pe", lambda e: e.matmul(out.ap, lhsT.ap, rhs.ap, start=start, stop=stop), rd=[lhsT, rhs], wr=[out])

    def TR(s, out, in_, ident):
        s.op("pe", lambda e: e.transpose(out.ap, in_.ap, ident.ap), rd=[in_, ident], wr=[out])

    def ACT(s, out, in_, func, bias=None, scale=1.0, accum=None):
        rd = [in_] + ([bias] if isinstance(bias, V) else []) + ([scale] if isinstance(scale, V) else [])
        wr = [out] + ([accum] if accum is not None else [])
        kw = {}
        if bias is not None:
            kw["bias"] = bias.ap if isinstance(bias, V) else bias
        if accum is not None:
            kw["accum_out"] = accum.ap
        sc = scale.ap if isinstance(scale, V) else scale
        s.op("act", lambda e: e.activation(out=out.ap, in_=in_.ap, func=func, scale=sc, **kw), rd=rd, wr=wr)

    def TT(s, out, a, b, op, eng="dve"):
        s.op(eng, lambda e: e.tensor_tensor(out=out.ap, in0=a.ap, in1=b.ap, op=op), rd=[a, b], wr=[out])

    def TS(s, out, a, s1, op0, s2=None, op1=None, accum=None, eng="dve"):
        rd = [a] + [x for x in (s1, s2) if isinstance(x, V)]
        wr = [out] + ([accum] if accum is not None else [])
        v1 = s1.ap if isinstance(s1, V) else s1
        v2 = s2.ap if isinstance(s2, V) else s2
        kw = {}
        if op1 is not None:
            kw["op1"] = op1
        if accum is not None:
            kw["accum_out"] = accum.ap
        s.op(eng, lambda e: e.tensor_scalar(out=out.ap, in0=a.ap, scalar1=v1, scalar2=v2, op0=op0, **kw), rd=rd, wr=wr)

    def STT(s, out, a, sc, b, op0, op1, eng="dve"):
        rd = [a, b] + ([sc] if isinstance(sc, V) else [])
        v = sc.ap if isinstance(sc, V) else sc
        s.op(eng, lambda e: e.scalar_tensor_tensor(out=out.ap, in0=a.ap, scalar=v, in1=b.ap, op0=op0, op1=op1), rd=rd, wr=[out])

    def CP(s, out, in_, eng=None):
        if eng is None:
            s.rr += 1
            eng = "act" if s.rr % 2 else "dve"
        if eng == "act":
            s.op("act", lambda e: e.activation(out=out.ap, in_=in_.ap, func=AF.Copy), rd=[in_], wr=[out])
        else:
            s.op(eng, lambda e: e.tensor_copy(out=out.ap, in_=in_.ap), rd=[in_], wr=[out])

    def MS(s, out, val, eng="dve"):
        s.op(eng, lambda e: e.memset(out.ap, val), wr=[out])

    def RCP(s, out, in_):
        s.op("dve", lambda e: e.reciprocal(out=out.ap, in_=in_.ap), rd=[in_], wr=[out])

    def RSUM(s, out, in_):
        s.op("dve", lambda e: e.reduce_sum(out=out.ap, in_=in_.ap, axis=AX.X), rd=[in_], wr=[out])

    def DMA(s, out, in_, q="sp"):
        s.op(q, lambda e: e.dma_start(out=out.ap, in_=in_.ap), rd=[in_], wr=[out], dma=True)

    def DMAs(s, out, in_, q="sp"):
        def f(e):
            with s.nc.allow_non_contiguous_dma(reason="small strided"):
                return e.dma_start(out=out.ap, in_=in_.ap)
        s.op(q, f, rd=[in_], wr=[out], dma=True)

    def psum(s, bf=False):
        if bf:
            s.pbi = (getattr(s, "pbi", -1) + 1) % len(s.psb)
            return s.psb[s.pbi]
        s.pfi = (getattr(s, "pfi", -1) + 1) % len(s.psf)
        return s.psf[s.pfi]


class KB2(KB):
    def setup(s):
        c = s.c
        s.A32, s.A16 = 19968, 60416
        s.arena32 = s.sb_raw("arena32", [128, s.A32], F32)
        s.arena16 = s.sb_raw("arena16", [128, s.A16], BF16)
        s.bar = s.sb_raw("bar", [128, 8], F32)
        s.identb = s.sb_raw("identb", [128, 128], BF16)
        s.identf = s.sb_raw("identf", [128, 128], F32)
        s.psf = [s.ps_raw(f"psf{i}", [128, 512], F32) for i in range(6)]
        s.psb = [s.ps_raw(f"psb{i}", [128, 1024], BF16) for i in range(2)]
        IN = "ExternalInput"
        OUT = "ExternalOutput"
        L = c.DEPTH
        s.x_in = s.dram("x_in", [c.NT, c.D], F32, IN)
        s.d_identb = s.dram("c_identb", [128, 128], BF16, IN)
        s.d_identf = s.dram("c_identf", [128, 128], F32, IN)
        s.w_in = s.dram("w_in", [L, c.D, c.INC], F32, IN)
        s.w_out = s.dram("w_out", [L, c.D, c.D], F32, IN)
        s.w_gate = s.dram("w_gate", [L, c.D, c.DFF], F32, IN)
        s.w_up = s.dram("w_up", [L, c.D, c.DFF], F32, IN)
        s.w_down = s.dram("w_down", [L, c.DFF, c.D], F32, IN)
        s.ln1 = s.dram("ln1", [L, c.D], F32, IN)
        s.ln2 = s.dram("ln2", [L, c.D], F32, IN)
        s.xres = s.dram("xres", [c.NT, c.D], F32, OUT)
        s.pT = s.dram("pT", [c.INC, c.NT], F32)
        s.mixT = s.dram("mixT", [c.D, c.NT], BF16)
        s._since = []
        s._bar = None
        s.barrier()
        s.DMA(tv(s.identb), tv(s.d_identb))
        s.DMA(tv(s.identf), tv(s.d_identf))
        s.blocks = []
        nb = c.T // c.TB
        for b in range(nb):
            toks = [(0, c.TB, b * c.TB)]
            if b == nb - 1:
                toks.append((c.TB, c.TS, c.T))
            s.blocks.append(toks)
        s.BW = c.TB + c.TS

    @staticmethod
    def tiles128(toks):
        out = []
        for (c0, n, g0) in toks:
            for o in range(0, n, 128):
                m = min(128, n - o)
                out.append((c0 + o, m, g0 + o))
        return out

    def norm_block(s, xsrc, toks, grow, actT, bufs):
        c = s.c
        xts, sq, xn, ss = bufs
        for i, (c0, n, g0) in enumerate(s.tiles128(toks)):
            xt = xts[i % 2]
            s.DMA(tv(xt)[:n], tv(xsrc)[g0:g0 + n, :])
            s.TT(tv(sq)[:n], tv(xt)[:n], tv(xt)[:n], ALU.mult)
            s.RSUM(tv(ss)[:n, 0:1], tv(sq)[:n])
            s.TS(tv(ss)[:n, 1:2], tv(ss)[:n, 0:1], 1.0 / c.D, ALU.mult, 1e-6, ALU.add)
            s.ACT(tv(ss)[:n, 2:3], tv(ss)[:n, 1:2], AF.Ln)
            s.ACT(tv(ss)[:n, 3:4], tv(ss)[:n, 2:3], AF.Exp, scale=-0.5)
            s.STT(tv(xn)[:n], tv(xt)[:n], tv(ss)[:n, 3:4], tv(grow)[:n], ALU.mult, ALU.mult)
            for k0 in range(0, c.KC, 8):
                kk = min(8, c.KC - k0)
                pb = s.psum(bf=True)
                for j in range(kk):
                    s.TR(tv(pb)[:, j * 128:j * 128 + n], tv(xn)[:n, (k0 + j) * 128:(k0 + j + 1) * 128], tv(s.identb)[:n, :n])
                s.CP(actT[:, k0:k0 + kk, c0:c0 + n], tv(pb)[:, 0:kk * 128].re("p (k t) -> p k t", k=kk)[:, :, 0:n])

    @staticmethod
    def fm_chunks(toks):
        merged = []
        for (c0, n, g0) in toks:
            if merged and merged[-1][0] + merged[-1][1] == c0 and merged[-1][2] + merged[-1][1] == g0:
                merged[-1] = (merged[-1][0], merged[-1][1] + n, merged[-1][2])
            else:
                merged.append((c0, n, g0))
        out = []
        for (c0, n, g0) in merged:
            k = -(-n // 512)
            step = -(-(-(-n // k)) // 64) * 64
            o = 0
            while o < n:
                m = min(step, n - o)
                out.append((c0 + o, m, g0 + o))
                o += m
        return out

    def wtile(s):
        s.wi = getattr(s, "wi", -1) + 1
        return s.wts[s.wi % len(s.wts)]

    def dense_fm(s, actT, toks, Wd, KC, col_lo, col_hi, epi):
        Wv = Wd.ap.rearrange("(kc p) c -> p kc c", p=128)
        for j0 in range(col_lo, col_hi, 128):
            cw = min(128, col_hi - j0)
            wt = s.wtile()
            wv = tv(wt).re("p (k c) -> p k c", c=128)
            s.DMA(wv[:, 0:KC, 0:cw], V(Wd.t, Wv[:, :, j0:j0 + cw]), q="pool")
            for (c0, n, g0) in s.fm_chunks(toks):
                ps = s.psum()
                for kc in range(KC):
                    s.MM(tv(ps)[:cw, :n], wv[:, kc, 0:cw], actT[:, kc, c0:c0 + n], start=(kc == 0), stop=(kc == KC - 1))
                epi(j0, cw, c0, n, g0, ps)

    def dense_tm(s, actT, toks, Wd, k_lo, nkc, epi):
        ncols = Wd.ap.shape[1]
        Wv = Wd.ap.rearrange("(kc p) c -> p kc c", p=128)
        tt = s.tiles128(toks)
        assert len(tt) <= 5
        for cb0 in range(0, ncols, 512):
            cb = min(512, ncols - cb0)
            pss = [s.psum() for _ in tt]
            for k0 in range(0, nkc, 8):
                kn = min(8, nkc - k0)
                wt = s.wtile()
                wv = tv(wt).re("p (k c) -> p k c", c=512)
                s.DMA(wv[:, 0:kn, 0:cb], V(Wd.t, Wv[:, k_lo + k0:k_lo + k0 + kn, cb0:cb0 + cb]), q="pool")
                for ti, (c0, n, g0) in enumerate(tt):
                    for kk in range(kn):
                        s.MM(tv(pss[ti])[:n, :cb], actT[:, k0 + kk, c0:c0 + n], wv[:, kk, 0:cb],
                             start=(k0 + kk == 0), stop=(k0 + kk == nkc - 1))
            for ti, (c0, n, g0) in enumerate(tt):
                epi(g0, n, cb0, cb, pss[ti])

    def resid_epi(s, stg):
        def epi(g0, n, cb0, cb, ps):
            s.sti = getattr(s, "sti", -1) + 1
            r = stg[s.sti % len(stg)]
            xr = s.xres.sub(g0)
            s.DMA(tv(r)[:n, :cb], tv(xr)[g0:g0 + n, cb0:cb0 + cb])
            s.TT(tv(r)[:n, :cb], tv(r)[:n, :cb], tv(ps)[:n, :cb], ALU.add)
            s.DMA(tv(xr)[g0:g0 + n, cb0:cb0 + cb], tv(r)[:n, :cb])
        return epi

    def grow_load(s, dst, src_row):
        s.DMA(tv(dst), V(src_row.t, src_row.ap.partition_broadcast(128)))

    def big_blocks(s, TBB):
        c = s.c
        TBB = min(TBB, c.T)
        nb = c.T // TBB
        out = []
        for b in range(nb):
            toks = [(o, min(512, TBB - o), b * TBB + o) for o in range(0, TBB, 512)]
            if b == nb - 1:
                toks.append((TBB, c.TS, c.T))
            out.append(toks)
        return out, TBB + c.TS

    def phase_win(s, l):
        c = s.c
        xsrc = s.x_in if l == 0 else s.xres
        blocks, BWw = s.big_blocks(1024)
        for bi, toks in enumerate(blocks):
            s.phase("win")
            actT = tv(s.a16(c.KC * BWw, "actT")).re("p (k t) -> p k t", t=BWw)
            s.wts = [s.a16(4096, f"w{i}") for i in range(3)]
            xn = s.a16(c.D, "xn")
            xts = [s.a32(c.D, f"xt{i}") for i in range(2)]
            sq = s.a32(c.D, "sq")
            grow = s.a32(c.D, "grow")
            ss = s.a32(8, "ss")
            stg = [s.a32(512, f"stg{i}") for i in range(3)]
            s.grow_load(grow, tv(s.ln1)[l])
            s.norm_block(xsrc, toks, grow, actT, (xts, sq, xn, ss))

            def epi(j0, cw, c0, n, g0, ps):
                s.sti = getattr(s, "sti", -1) + 1
                r = stg[s.sti % len(stg)]
                s.CP(tv(r)[:cw, :n], tv(ps)[:cw, :n])
                s.DMA(tv(s.pT.sub((j0, g0)))[j0:j0 + cw, g0:g0 + n], tv(r)[:cw, :n])
            s.dense_fm(actT, toks, tv(s.w_in)[l], c.KC, 0, c.INC, epi)

    def phase_wout(s, l):
        c = s.c
        for bi, toks in enumerate(s.blocks):
            s.phase("wout")
            actT = tv(s.a16(c.KC * s.BW, "actT")).re("p (k t) -> p k t", t=s.BW)
            s.wts = [s.a16(4096, f"w{i}") for i in range(3)]
            stg = [s.a32(512, f"stg{i}") for i in range(4)]
            mv = s.mixT.h.rearrange("(k p) t -> p k t", p=128)
            for (c0, n, g0) in toks:
                s.DMA(actT[:, :, c0:c0 + n], V(s.mixT, mv[:, :, g0:g0 + n]))
            if l == 0 and bi == 0:
                pass
            s.dense_tm(actT, toks, tv(s.w_out)[l], 0, c.KC, s.resid_epi(stg))

    def phase_ffn(s, l):
        c = s.c
        nh = 2
        fh = c.FC // nh
        assert fh * nh == c.FC
        for bi, toks in enumerate(s.blocks):
            s.phase("ffn")
            actT = tv(s.a16(c.KC * s.BW, "actT")).re("p (k t) -> p k t", t=s.BW)
            hidT = tv(s.a16(fh * s.BW, "hidT")).re("p (k t) -> p k t", t=s.BW)
            s.wts = [s.a16(4096, f"w{i}") for i in range(4)]
            xn = s.a16(c.D, "xn")
            xts = [s.a32(c.D, f"xt{i}") for i in range(2)]
            sq = s.a32(c.D, "sq")
            grow = s.a32(c.D, "grow")
            ss = s.a32(8, "ss")
            stg = [s.a32(512, f"stg{i}") for i in range(4)]
            sgs = [s.a32(512, f"sg{i}") for i in range(2)]
            s.grow_load(grow, tv(s.ln2)[l])
            s.norm_block(s.xres, toks, grow, actT, (xts, sq, xn, ss))
            Wg = tv(s.w_gate)[l]
            Wu = tv(s.w_up)[l]
            Wgv = Wg.ap.rearrange("(kc p) c -> p kc c", p=128)
            Wuv = Wu.ap.rearrange("(kc p) c -> p kc c", p=128)
            for h in range(nh):
                for fj in range(fh):
                    f0 = (h * fh + fj) * 128
                    wg = tv(s.wtile()).re("p (k c) -> p k c", c=128)
                    wu = tv(s.wtile()).re("p (k c) -> p k c", c=128)
                    s.DMA(wg[:, 0:c.KC, :], V(s.w_gate, Wgv[:, :, f0:f0 + 128]), q="pool")
                    s.DMA(wu[:, 0:c.KC, :], V(s.w_up, Wuv[:, :, f0:f0 + 128]), q="pool")
                    for (c0, n, g0) in s.fm_chunks(toks):
                        pg = s.psum()
                        pu = s.psum()
                        for kc in range(c.KC):
                            s.MM(tv(pg)[:, :n], wg[:, kc, :], actT[:, kc, c0:c0 + n], start=(kc == 0), stop=(kc == c.KC - 1))
                        for kc in range(c.KC):
                            s.MM(tv(pu)[:, :n], wu[:, kc, :], actT[:, kc, c0:c0 + n], start=(kc == 0), stop=(kc == c.KC - 1))
                        s.sgi = getattr(s, "sgi", -1) + 1
                        sg = sgs[s.sgi % 2]
                        s.ACT(tv(sg)[:, :n], tv(pg)[:, :n], AF.Silu)
                        s.TT(hidT[:, fj, c0:c0 + n], tv(sg)[:, :n], tv(pu)[:, :n], ALU.mult)
                s.dense_tm(hidT, toks, tv(s.w_down)[l], h * fh, fh, s.resid_epi(stg))


def np_dt(dt):
    return {F32: np.float32, BF16: ml_dtypes.bfloat16, I32: np.int32}[dt]


class KB3(KB2):
    def chunk_la(s, arr, gC, S_f, S_b, yT, n, C, hp, delta, bufs, mask):
        dh = 128 // hp
        nch = n // C
        rt, kt, kh, vT = arr["rt"], arr["kt"], arr["kh"], arr["vT"]
        NM = 5 if delta else 1
        Mh, tok, Vp, Up, R_f, R_b, Xs = bufs["Mh"], bufs["tok"], bufs["Vp"], bufs["Up"], bufs["R_f"], bufs["R_b"], bufs["Xs"]
        nlev = int(np.log2(C)) if C > 1 else 0
        for ci in range(nch):
            cs = slice(ci * C, (ci + 1) * C)
            for h in range(hp):
                hs = slice(h * dh, (h + 1) * dh)
                pm = s.psum()
                if delta:
                    at, bt = arr["at"], arr["bt"]
                    pairs = [(bt, at), (bt, rt), (kt, at), (kt, rt), (at, bt)]
                else:
                    pairs = [(kt, rt)]
                for i, (lt, rh) in enumerate(pairs):
                    s.MM(tv(pm)[:C, i * C:(i + 1) * C], lt[hs, cs], rh[hs, cs])
                s.TT(tv(Mh[h])[:C, :NM * C].re("p (m c) -> p m c", c=C), tv(pm)[:C, :NM * C].re("p (m c) -> p m c", c=C),
                     mask[:C, :, :C], ALU.mult)
            yield
            pb = s.psum(bf=True)
            srcs = ([arr["bh"]] if delta else []) + [kh, vT]
            for i, a in enumerate(srcs):
                s.TR(tv(pb)[:C, i * 128:(i + 1) * 128], a[:, cs], tv(s.identb))
            nt = len(srcs)
            s.CP(tv(tok)[:C, :nt * 128], tv(pb)[:C, :nt * 128])
            tk = tv(tok)[:C, (nt - 2) * 128:(nt - 1) * 128]
            tvv = tv(tok)[:C, (nt - 1) * 128:nt * 128]
            tb = tv(tok)[:C, 0:128] if delta else None
            if hp > 1:
                for h in range(hp):
                    hs = slice(h * dh, (h + 1) * dh)
                    s.CP(tv(Vp[h])[:C, hs], tvv[:, hs])
                vps = [tv(Vp[h])[:C, :] for h in range(hp)]
            else:
                vps = [tvv]
            yield
            if delta:
                pr = s.psum()
                for h in range(hp):
                    hs = slice(h * dh, (h + 1) * dh)
                    s.MM(tv(pr)[:C, hs], arr["at"][:, cs], tv(S_b)[:, hs], start=True, stop=False)
                    s.MM(tv(pr)[:C, hs], tv(Mh[h])[:C, 2 * C:3 * C], tvv[:, hs], start=False, stop=True)
                s.CP(tv(R_f)[:C, :], tv(pr)[:C, 0:128], eng="dve")
                s.CP(tv(R_b)[:C, :], tv(pr)[:C, 0:128], eng="act")
                yield
                X = [tv(Mh[h])[:C, 4 * C:5 * C] for h in range(hp)]
                XT = [tv(Mh[h])[:C, 0:C] for h in range(hp)]
                for lev in range(nlev):
                    pa = s.psum()
                    for h in range(hp):
                        hs = slice(h * dh, (h + 1) * dh)
                        s.MM(tv(pa)[:C, hs], XT[h], tv(R_b)[:C, hs])
                    px = None
                    if lev < nlev - 1:
                        px = s.psum()
                        for h in range(hp):
                            s.MM(tv(px)[:C, (2 * h) * C:(2 * h + 1) * C], XT[h], X[h])
                            s.MM(tv(px)[:C, (2 * h + 1) * C:(2 * h + 2) * C], X[h], XT[h])
                    s.TT(tv(R_f)[:C, :], tv(R_f)[:C, :], tv(pa)[:C, 0:128], ALU.add)
                    s.CP(tv(R_b)[:C, :], tv(R_f)[:C, :], eng="act")
                    if px is not None:
                        xs = Xs[lev % 2]
                        s.CP(tv(xs)[:C, :2 * hp * C], tv(px)[:C, :2 * hp * C], eng="dve")
                        X = [tv(xs)[:C, (2 * h) * C:(2 * h + 1) * C] for h in range(hp)]
                        XT = [tv(xs)[:C, (2 * h + 1) * C:(2 * h + 2) * C] for h in range(hp)]
                    yield
                if hp > 1:
                    for h in range(hp):
                        hs = slice(h * dh, (h + 1) * dh)
                        s.CP(tv(Up[h])[:C, hs], tv(R_b)[:C, hs])
                    ups = [tv(Up[h])[:C, :] for h in range(hp)]
                else:
                    ups = [tv(R_b)[:C, :]]
            py = s.psum()
            s.MM(tv(py)[:, :C], tv(S_b), rt[:, cs], start=True, stop=False)
            for h in range(hp):
                last = (h == hp - 1)
                if delta:
                    s.MM(tv(py)[:, :C], ups[h], tv(Mh[h])[:C, 1 * C:2 * C], start=False, stop=False)
                    s.MM(tv(py)[:, :C], vps[h], tv(Mh[h])[:C, 3 * C:4 * C], start=False, stop=last)
                else:
                    s.MM(tv(py)[:, :C], vps[h], tv(Mh[h])[:C, 0:C], start=False, stop=last)
            s.CP(yT[:, cs], tv(py)[:, :C])
            yield
            pS = s.psum()
            for h in range(hp):
                hs = slice(h * dh, (h + 1) * dh)
                if delta:
                    s.MM(tv(pS)[:, hs], tb, tv(R_b)[:C, hs], start=True, stop=False)
                    s.MM(tv(pS)[:, hs], tk, tvv[:, hs], start=False, stop=True)
                else:
                    s.MM(tv(pS)[:, hs], tk, tvv[:, hs], start=True, stop=True)
            for h in range(hp):
                hs = slice(h * dh, (h + 1) * dh)
                s.STT(tv(S_f)[hs, :], tv(S_f)[hs, :], gC[hs, ci:ci + 1], tv(pS)[hs, hs], ALU.mult, ALU.add)
                s.CP(tv(S_b)[hs, hs], tv(S_f)[hs, :], eng="act")
            yield

    def run_interleaved(s, gens):
        gens = list(gens)
        while gens:
            nxt = []
            for g in gens:
                try:
                    next(g)
                    nxt.append(g)
                except StopIteration:
                    pass
            gens = nxt

    def chunk_cumsum(s, a, b, n, C):
        d = 1
        while d < C:
            av = tv(a)[:, :n].re("p (c t) -> p c t", t=C)
            bv = tv(b)[:, :n].re("p (c t) -> p c t", t=C)
            s.TT(bv[:, :, d:], av[:, :, d:], av[:, :, :C - d], ALU.add)
            s.CP(bv[:, :, :d], av[:, :, :d], eng="dve")
            a, b = b, a
            d *= 2
        return a, b

    def la_bufs(s, hp, delta, tag):
        C = 64
        return dict(
            Mh=[s.a16(5 * C, f"Mh{tag}{h}", 128) for h in range(hp)],
            tok=s.a16(3 * 128, f"tok{tag}"),
            Vp=[s.a16(128, f"Vp{tag}{h}") for h in range(hp)],
            Up=[s.a16(128, f"Up{tag}{h}") for h in range(hp)],
            R_f=s.a32(128, f"Rf{tag}"), R_b=s.a16(128, f"Rb{tag}"),
            Xs=[s.a16(4 * C, f"Xs{tag}{i}") for i in range(2)])

    def segs(s):
        c = s.c
        out = []
        nb = c.T // c.TB
        for b in range(nb):
            out.append((b * c.TB, c.TB, c.CH, 0, b == 0, b == nb - 1))
        out.append((c.T, c.TS, c.TS, 1, True, True))
        return out


class KB4(KB3):
    def setup2(s):
        c = s.c
        IN, OUT = "ExternalInput", "ExternalOutput"
        L = c.DEPTH
        s.mask3 = s.sb_raw("mask3", [64, 5, 64], BF16)
        s.onesf = s.sb_raw("onesf", [128, 128], F32)
        s.blkf = s.sb_raw("blkf", [128, 128], F32)
        s.d_mask3 = s.dram("c_mask3", [64, 5, 64], BF16, IN)
        s.d_onesf = s.dram("c_onesf", [128, 128], F32, IN)
        s.d_blkf = s.dram("c_blkf", [128, 128], F32, IN)
        s.DMA(tv(s.mask3), tv(s.d_mask3))
        s.DMA(tv(s.onesf), tv(s.d_onesf))
        s.DMA(tv(s.blkf), tv(s.d_blkf))
        s.hgrn_lb = s.dram("hgrn_lb", [L, c.HGW], F32, IN)
        s.hgrn_norm = s.dram("hgrn_norm", [L, c.HGW], F32, IN)
        s.st_hg = s.dram("st_hg", [L, c.HGH, 128, 128], F32, IN)
        s.st_rw = s.dram("st_rw", [L, c.RWH, 64, 64], F32, IN)
        s.st_sh = s.dram("st_sh", [L, c.RWC], F32, IN)
        for nm, shp in [("rwkv_mu", [L, c.RWC]), ("rwkv_w0", [L, c.RWW]), ("rwkv_w2", [L, 64, c.RWW]),
                        ("rwkv_a0", [L, c.RWW]), ("rwkv_a2", [L, 64, c.RWW]), ("rwkv_g2", [L, 160, c.RWW]),
                        ("rwkv_kk", [L, c.RWW]), ("rwkv_ka", [L, c.RWW]), ("rwkv_rk", [L, c.RWW]),
                        ("rwkv_lnx_w", [L, c.RWW]), ("rwkv_lnx_b", [L, c.RWW]),
                        ("q_norm", [L, 128]), ("k_norm", [L, 128])]:
            setattr(s, nm, s.dram(nm, shp, F32, IN))
        s.hg_o = s.dram("hg_o", [L, 2, c.HGH, 128, 128], F32, OUT)
        s.rw_o = s.dram("rw_o", [L, 2, c.RWH, 64, 64], F32, OUT)
        s.sh_o = s.dram("sh_o", [L, 2, c.RWC], F32, OUT)

    def col_load(s, dst, src_vec, ncol):
        s.DMAs(dst, V(src_vec.t, src_vec.ap.rearrange("(c p) -> p c", p=128)))

    def sigmoid_(s, t, n, parts=128):
        s.ACT(t, t, AF.Exp, scale=-1.0)
        s.TS(t, t, 1.0, ALU.add)
        s.RCP(t, t)

    def phase_hgrn(s, l):
        c = s.c
        H = c.HGH
        s.phase("hg")
        TBm = c.TB
        prm = s.a32(4 * H, "prm")
        s.col_load(tv(prm)[:, 2 * H:3 * H], tv(s.hgrn_norm)[l], H)
        if l == 0:
            s.MS(tv(prm)[:, 0:H], 0.0)
        else:
            s.col_load(tv(prm)[:, 0:H], tv(s.hgrn_lb)[0], H)
            s.col_load(tv(prm)[:, 3 * H:4 * H], tv(s.hgrn_lb)[1], H)
            s.TT(tv(prm)[:, 0:H], tv(prm)[:, 0:H], tv(prm)[:, 3 * H:4 * H], ALU.subtract)
            s.ACT(tv(prm)[:, 0:H], tv(prm)[:, 0:H], AF.Exp)
            s.TS(tv(prm)[:, 0:H], tv(prm)[:, 0:H], 1.0, ALU.add)
            s.RCP(tv(prm)[:, 0:H], tv(prm)[:, 0:H])
        s.TS(tv(prm)[:, H:2 * H], tv(prm)[:, 0:H], -1.0, ALU.mult, 1.0, ALU.add)
        tmp = [s.a32(TBm, f"t{i}") for i in range(6)]
        arrs = [{k: tv(s.a16(TBm, f"{k}{h}")) for k in ("rt", "kt", "kh", "vT")} for h in range(H)]
        yTs = [s.a32(TBm, f"yT{h}") for h in range(H)]
        gCs = [s.a32(8, f"gC{h}") for h in range(H)]
        S_f = [s.a32(128, f"Sf{h}") for h in range(H)]
        S_b = [s.a16(128, f"Sb{h}") for h in range(H)]
        bufs = [s.la_bufs(1, False, f"h{h}") for h in range(H)]
        o16 = [s.a16(TBm, f"o16{i}") for i in range(2)]
        for (g0, n, C, seq, first, last) in s.segs():
            nch = n // C
            for h in range(H):
                if first:
                    if seq == 0:
                        s.MS(tv(S_f[h]), 0.0)
                    else:
                        s.DMA(tv(S_f[h]), tv(s.st_hg)[l, h])
                    s.CP(tv(S_b[h]), tv(S_f[h]))
                t = [tv(x)[:, :n] for x in tmp]
                lb = tv(prm)[:, h:h + 1]
                oml = tv(prm)[:, H + h:H + h + 1]
                s.DMA(t[0], tv(s.pT)[c.c_hq + h * 128:c.c_hq + (h + 1) * 128, g0:g0 + n])
                s.DMA(t[1], tv(s.pT)[c.c_hf + h * 128:c.c_hf + (h + 1) * 128, g0:g0 + n])
                s.ACT(t[2], t[1], AF.Exp, scale=-1.0)
                s.TS(t[3], t[2], 1.0, ALU.add)
                s.RCP(t[3], t[3])
                s.TS(t[4], t[2], lb, ALU.mult, 1.0, ALU.add)
                s.TT(t[4], t[4], t[3], ALU.mult)
                s.ACT(t[4], t[4], AF.Ln)
                s.TT(t[2], t[2], t[3], ALU.mult)
                s.TS(t[2], t[2], oml, ALU.mult)
                s.ACT(t[3], t[0], AF.Exp, scale=-1.0)
                s.TS(t[3], t[3], 1.0, ALU.add)
                s.RCP(t[3], t[3])
                s.TT(t[0], t[0], t[3], ALU.mult)
                bT, oT = s.chunk_cumsum(tmp[4], tmp[5], n, C)
                b = tv(bT)[:, :n]
                b3 = b.re("p (c t) -> p c t", t=C)
                t3v = t[3].re("p (c t) -> p c t", t=C)
                s.ACT(t[3], b, AF.Exp)
                s.STT(arrs[h]["rt"][:, :n], t[0], 128.0 ** -0.5, t[3], ALU.mult, ALU.mult)
                s.ACT(t[3], b, AF.Exp, scale=-1.0)
                s.TT(arrs[h]["kt"][:, :n], t[2], t[3], ALU.mult)
                s.TT(t3v, V(b.t, b3.ap[:, :, C - 1:C].to_broadcast([128, nch, C])), b3, ALU.subtract)
                s.ACT(t[3], t[3], AF.Exp)
                s.TT(arrs[h]["kh"][:, :n], t[2], t[3], ALU.mult)
                s.ACT(tv(gCs[h])[:, :nch], V(b.t, b3.ap[:, :, C - 1]), AF.Exp)
                s.DMA(t[0], tv(s.pT)[c.c_hi + h * 128:c.c_hi + (h + 1) * 128, g0:g0 + n])
                s.CP(arrs[h]["vT"][:, :n], t[0])
            s.run_interleaved([s.chunk_la({k: v[:, :n] for k, v in arrs[h].items()}, tv(gCs[h]), S_f[h], S_b[h],
                                          tv(yTs[h])[:, :n], n, C, 1, False, bufs[h], tv(s.mask3)[:, 1:2, :])
                               for h in range(H)])
            for h in range(H):
                t = [tv(x)[:, :n] for x in tmp]
                y = tv(yTs[h])[:, :n]
                s.TT(t[0], y, y, ALU.mult)
                ps = s.psum()
                s.MM(tv(ps)[:, :n], tv(s.onesf), t[0])
                s.TS(t[1], tv(ps)[:, :n], 1.0 / 128, ALU.mult, 1e-6, ALU.add)
                s.ACT(t[1], t[1], AF.Ln)
                s.ACT(t[1], t[1], AF.Exp, scale=-0.5)
                s.TT(t[0], y, t[1], ALU.mult)
                s.DMA(t[2], tv(s.pT)[c.c_hg + h * 128:c.c_hg + (h + 1) * 128, g0:g0 + n])
                s.sigmoid_(t[2], n)
                ob = tv(o16[h % 2])[:, :n]
                s.STT(ob, t[0], tv(prm)[:, 2 * H + h:2 * H + h + 1], t[2], ALU.mult, ALU.mult)
                s.DMA(tv(s.mixT.sub(("hg", h, g0)))[h * 128:(h + 1) * 128, g0:g0 + n], ob)
                if last:
                    s.DMA(tv(s.hg_o)[l, seq, h], tv(S_f[h]))


def consts():
    m = np.zeros((64, 5, 64), np.float32)
    p = np.arange(64)[:, None]
    j = np.arange(64)[None, :]
    m[:, 0] = p < j
    m[:, 1] = p <= j
    m[:, 2] = p < j
    m[:, 3] = p <= j
    m[:, 4] = j < p
    blk = np.zeros((128, 128), np.float32)
    blk[:64, :64] = 1
    blk[64:, 64:] = 1
    return dict(c_identb=np.eye(128).astype(ml_dtypes.bfloat16), c_identf=np.eye(128, dtype=np.float32),
                c_mask3=m.astype(ml_dtypes.bfloat16), c_onesf=np.ones((128, 128), np.float32), c_blkf=blk)


def make_in_maps(cfg, inp, ncores=8):
    L = cfg.DEPTH
    B = inp["x_prompt"].shape[0]
    cs = consts()
    maps = []
    shared = {}
    for k in ("w_in", "w_out", "w_gate", "w_up", "w_down", "ln1", "ln2", "hgrn_lb", "hgrn_norm", "rwkv_mu", "rwkv_w0",
              "rwkv_w2", "rwkv_a0", "rwkv_a2", "rwkv_g2", "rwkv_kk", "rwkv_ka", "rwkv_lnx_w", "rwkv_lnx_b", "q_norm", "k_norm"):
        shared[k] = np.ascontiguousarray(inp[k])
    shared["rwkv_rk"] = np.ascontiguousarray(inp["rwkv_rk"]).reshape(L, cfg.RWW)
    for i in range(L):
        shared[f"cache_k{i}"] = np.ascontiguousarray(inp["cache_k"][i]).reshape(cfg.NPOOL * 128, cfg.KVW)
        shared[f"cache_v{i}"] = np.ascontiguousarray(inp["cache_v"][i]).reshape(cfg.NPOOL * 128, cfg.KVW)
        shared[f"cache_kidx{i}"] = np.ascontiguousarray(inp["cache_kidx"][i]).reshape(cfg.NPOOL * 128, 128)
    for ci in range(ncores):
        m = dict(shared)
        m.update(cs)
        m["x_in"] = np.concatenate([inp["x_prompt"][ci % B], inp["x_sample"][ci]], axis=0)
        m["st_hg"] = np.ascontiguousarray(inp["state_hgrn"][:, ci])
        m["st_rw"] = np.ascontiguousarray(inp["state_rwkv"][:, ci])
        m["st_sh"] = np.ascontiguousarray(inp["state_shift"][:, ci])
        m["ptab"] = np.ascontiguousarray(inp["page_table"][ci:ci + 1]).astype(np.int32)
        maps.append(m)
    return maps


import os
DBG = os.environ.get('RWDBG', '').split(',')


class KB5(KB4):
    def rw_xs(s, dst, tP, l, ch, nr, g0, n, seq, first, mu):
        c = s.c
        r0 = c.c_rw + ch * 128
        P = tv(tP)
        if first and seq == 0:
            s.MS(P[:nr, 0:1], 0.0)
            s.DMA(P[:nr, 1:n + 1], tv(s.pT)[r0:r0 + nr, g0:g0 + n])
        elif first:
            src = tv(s.st_sh)[l, ch * 128:ch * 128 + nr]
            s.DMAs(P[:nr, 0:1], V(src.t, src.ap.rearrange("(r o) -> r o", o=1)))
            s.DMA(P[:nr, 1:n + 1], tv(s.pT)[r0:r0 + nr, g0:g0 + n])
        else:
            s.DMA(P[:nr, 0:n + 1], tv(s.pT)[r0:r0 + nr, g0 - 1:g0 + n])
        s.TT(dst[:nr], P[:nr, 0:n], P[:nr, 1:n + 1], ALU.subtract)
        s.STT(dst[:nr], dst[:nr], mu[:nr, ch:ch + 1], P[:nr, 1:n + 1], ALU.mult, ALU.add)

    def phase_rwkv(s, l):
        c = s.c
        G = c.RWW // 128
        s.phase("rw")
        TBm = c.TB
        NF = c.RWC // 128
        mu = s.a32(NF + 1, "mu")
        s.col_load(tv(mu)[:, 0:NF], tv(s.rwkv_mu)[l, 0:NF * 128], NF)
        srcm = tv(s.rwkv_mu)[l, NF * 128:c.RWC]
        s.DMAs(tv(mu)[:32, NF:NF + 1], V(srcm.t, srcm.ap.rearrange("(r o) -> r o", o=1)))
        prm = s.a32(9 * G, "prm")
        names = ["rwkv_w0", "rwkv_a0", "rwkv_kk", "rwkv_ka", "rwkv_rk", "rwkv_lnx_w", "rwkv_lnx_b"]
        pc = {}
        for i, nm in enumerate(names):
            s.col_load(tv(prm)[:, i * G:(i + 1) * G], tv(getattr(s, nm))[l], G)
            pc[nm] = tv(prm)[:, i * G:(i + 1) * G]
        s.TS(tv(prm)[:, 7 * G:8 * G], pc["rwkv_w0"], -1.0, ALU.mult)
        s.TS(tv(prm)[:, 8 * G:9 * G], pc["rwkv_a0"], -1.0, ALU.mult)
        negw0 = tv(prm)[:, 7 * G:8 * G]
        nega0 = tv(prm)[:, 8 * G:9 * G]
        w2a2 = s.a16(c.RWW, "w2a2")
        g2A = s.a16(c.RWW, "g2A")
        g2B = s.a16(c.RWW, "g2B")
        s.DMA(tv(w2a2)[0:64, :], tv(s.rwkv_w2)[l], q="pool")
        s.DMA(tv(w2a2)[64:128, :], tv(s.rwkv_a2)[l], q="pool")
        s.DMA(tv(g2A), tv(s.rwkv_g2)[l, 0:128, :], q="pool")
        s.DMA(tv(g2B)[0:32, :], tv(s.rwkv_g2)[l, 128:160, :], q="pool")
        lin = s.a16(TBm, "lin")
        sgA = s.a16(TBm, "sgA")
        sgB = s.a16(TBm, "sgB")
        tmp = [s.a32(TBm + 8, f"t{i}") for i in range(11)]
        keys = ("rt", "kt", "kh", "vT", "at", "bt", "bh")
        arrs = [{k: tv(s.a16(TBm, f"{k}{j}")) for k in keys} for j in range(G)]
        yTs = [s.a32(TBm, f"yT{j}") for j in range(G)]
        bon = [s.a32(TBm, f"bon{j}") for j in range(G)]
        gg = [s.a32(TBm, f"gg{j}") for j in range(G)]
        gCs = [s.a32(8, f"gC{j}") for j in range(G)]
        S_f = [s.a32(64, f"Sf{j}") for j in range(G)]
        S_b = [s.a16(128, f"Sb{j}") for j in range(G)]
        bufs = [s.la_bufs(2, True, f"r{j}") for j in range(G)]
        o16 = [s.a16(TBm, f"o16{i}") for i in range(2)]
        for j in range(G):
            for h in range(2):
                s.MS(tv(bufs[j]["Vp"][h]), 0.0)
                s.MS(tv(bufs[j]["Up"][h]), 0.0)
        for (g0, n, C, seq, first, last) in s.segs():
            nch = n // C
            t = [tv(x)[:, :n] for x in tmp]
            tP = tmp[10]
            s.rw_xs(t[0], tP, l, 3 * G, 128, g0, n, seq, first, tv(mu))
            s.ACT(t[1][0:64], t[0][0:64], AF.Exp, scale=-2.0)
            s.TS(t[1][0:64], t[1][0:64], 1.0, ALU.add)
            s.RCP(t[1][0:64], t[1][0:64])
            s.TS(tv(lin)[0:64, :n], t[1][0:64], 2.0, ALU.mult, -1.0, ALU.add)
            s.CP(tv(lin)[64:128, :n], t[0][64:128])
            s.rw_xs(t[0], tP, l, 3 * G + 1, 128, g0, n, seq, first, tv(mu))
            s.sigmoid_(t[0], n)
            s.CP(tv(sgA)[:, :n], t[0])
            s.rw_xs(t[0], tP, l, 3 * G + 2, 32, g0, n, seq, first, tv(mu))
            s.sigmoid_(t[0][0:32], n)
            s.CP(tv(sgB)[0:32, :n], t[0][0:32])
            for j in range(G):
                js = slice(j * 128, (j + 1) * 128)
                if first:
                    s.MS(tv(S_b[j]), 0.0)
                    if seq == 0 or "st" in DBG:
                        s.MS(tv(S_f[j]), 0.0)
                    else:
                        src = tv(s.st_rw)[l, 2 * j:2 * j + 2]
                        s.DMA(tv(tmp[9])[0:64, 0:128].re("v (h k) -> v h k", h=2), V(src.t, src.ap.rearrange("h v k -> v h k")))
                        pt = s.psum()
                        s.TR(tv(pt)[:, 0:64], tv(tmp[9])[0:64, 0:128], tv(s.identf)[0:64, 0:64])
                        s.CP(tv(S_f[j]), tv(pt)[:, 0:64], eng="dve")
                        for h in range(2):
                            hs = slice(h * 64, (h + 1) * 64)
                            s.CP(tv(S_b[j])[hs, hs], tv(S_f[j])[hs, :], eng="act")
                tr, tk, tvv, tlw, ta, tkk, tb = t[0], t[1], t[2], t[3], t[4], t[5], t[6]
                s.rw_xs(tr, tP, l, j, 128, g0, n, seq, first, tv(mu))
                s.rw_xs(tk, tP, l, G + j, 128, g0, n, seq, first, tv(mu))
                s.rw_xs(tvv, tP, l, 2 * G + j, 128, g0, n, seq, first, tv(mu))
                pw = s.psum()
                s.MM(tv(pw)[:, :n], tv(w2a2)[0:64, js], tv(lin)[0:64, :n])
                pa = s.psum()
                s.MM(tv(pa)[:, :n], tv(w2a2)[64:128, js], tv(lin)[64:128, :n])
                pg = s.psum()
                s.MM(tv(pg)[:, :n], tv(g2A)[:, js], tv(sgA)[:, :n], start=True, stop=False)
                s.MM(tv(pg)[:, :n], tv(g2B)[0:32, js], tv(sgB)[0:32, :n], start=False, stop=True)
                s.ACT(tlw, tv(pw)[:, :n], AF.Exp, bias=negw0[:, j:j + 1], scale=-1.0)
                s.TS(tlw, tlw, 1.0, ALU.add)
                s.RCP(tlw, tlw)
                s.TS(tlw, tlw, -0.6065306597126334, ALU.mult)
                s.ACT(ta, tv(pa)[:, :n], AF.Exp, bias=nega0[:, j:j + 1], scale=-1.0)
                s.TS(ta, ta, 1.0, ALU.add)
                s.RCP(ta, ta)
                s.CP(tv(gg[j])[:, :n], tv(pg)[:, :n])
                s.TS(tkk, tk, pc["rwkv_kk"][:, j:j + 1], ALU.mult)
                s.TT(t[7], tkk, tkk, ALU.mult)
                ps = s.psum()
                s.MM(tv(ps)[:, :n], tv(s.blkf), t[7])
                s.TS(t[7], tv(ps)[:, :n], 1e-24, ALU.max)
                s.ACT(t[7], t[7], AF.Ln)
                s.ACT(t[7], t[7], AF.Exp, scale=-0.5)
                s.TT(tkk, tkk, t[7], ALU.mult)
                s.TS(t[7], ta, -1.0, ALU.add, pc["rwkv_ka"][:, j:j + 1], ALU.mult)
                s.TS(t[7], t[7], 1.0, ALU.add)
                s.TT(tk, tk, t[7], ALU.mult)
                s.TT(tb, tkk, ta, ALU.mult)
                s.STT(t[7], tr, pc["rwkv_rk"][:, j:j + 1], tk, ALU.mult, ALU.mult)
                ps = s.psum()
                s.MM(tv(ps)[:, :n], tv(s.blkf), t[7])
                s.TT(tv(bon[j])[:, :n], tv(ps)[:, :n], tvv, ALU.mult)
                s.CP(t[7], tlw, eng="dve")
                bT, oT = s.chunk_cumsum(tmp[7], tmp[8], n, C)
                b = tv(bT)[:, :n]
                e = tv(oT)[:, :n]
                b3 = b.re("p (c t) -> p c t", t=C)
                e3 = e.re("p (c t) -> p c t", t=C)
                A = arrs[j]
                s.ACT(e, b, AF.Exp)
                s.TT(A["rt"][:, :n], tr, e, ALU.mult)
                s.ACT(e, b, AF.Exp, scale=-1.0)
                s.TT(A["kt"][:, :n], tk, e, ALU.mult)
                s.TT(A["bt"][:, :n], tb, e, ALU.mult)
                s.TT(e, b, tlw, ALU.subtract)
                s.ACT(e, e, AF.Exp)
                s.STT(A["at"][:, :n], tkk, -1.0, e, ALU.mult, ALU.mult)
                s.TT(e3, V(b.t, b3.ap[:, :, C - 1:C].to_broadcast([128, nch, C])), b3, ALU.subtract)
                s.ACT(e, e, AF.Exp)
                s.TT(A["kh"][:, :n], tk, e, ALU.mult)
                s.TT(A["bh"][:, :n], tb, e, ALU.mult)
                s.ACT(tv(gCs[j])[:, :nch], V(b.t, b3.ap[:, :, C - 1]), AF.Exp)
                s.CP(A["vT"][:, :n], tvv)
            if "la" not in DBG:
                s.run_interleaved([s.chunk_la({k: v[:, :n] for k, v in arrs[j].items()}, tv(gCs[j]), S_f[j], S_b[j],
                                              tv(yTs[j])[:, :n], n, C, 2, True, bufs[j], tv(s.mask3))
                                   for j in range(G)])
            for j in range(G):
                y = tv(yTs[j])[:, :n]
                ps = s.psum()
                s.MM(tv(ps)[:, :n], tv(s.blkf), y)
                s.STT(t[0], tv(ps)[:, :n], -1.0 / 64, y, ALU.mult, ALU.add)
                s.TT(t[1], t[0], t[0], ALU.mult)
                ps2 = s.psum()
                s.MM(tv(ps2)[:, :n], tv(s.blkf), t[1])
                s.TS(t[1], tv(ps2)[:, :n], 1.0 / 64, ALU.mult, 64e-5, ALU.add)
                s.ACT(t[1], t[1], AF.Ln)
                s.ACT(t[1], t[1], AF.Exp, scale=-0.5)
                s.TT(t[0], t[0], t[1], ALU.mult)
                s.TS(t[0], t[0], pc["rwkv_lnx_w"][:, j:j + 1], ALU.mult, pc["rwkv_lnx_b"][:, j:j + 1], ALU.add)
                s.TT(t[0], t[0], tv(bon[j])[:, :n], ALU.add)
                ob = tv(o16[j % 2])[:, :n]
                s.TT(ob, t[0], tv(gg[j])[:, :n], ALU.mult)
                r0 = c.HGW + j * 128
                s.DMA(tv(s.mixT.sub(("rw", j, g0)))[r0:r0 + 128, g0:g0 + n], ob)
                if last and "st" not in DBG:
                    pt = s.psum()
                    s.TR(tv(pt)[0:64, 0:128], tv(S_f[j]), tv(s.identf))
                    s.CP(tv(tmp[9])[0:64, 0:128], tv(pt)[0:64, 0:128], eng="dve")
                    dst = tv(s.rw_o)[l, seq, 2 * j:2 * j + 2]
                    s.DMA(V(dst.t, dst.ap.rearrange("h v k -> v h k")), tv(tmp[9])[0:64, 0:128].re("v (h k) -> v h k", h=2))
            if last and "sh" not in DBG:
                gl = g0 + n - 1
                dst = tv(s.sh_o)[l, seq]
                s.DMAs(V(dst.t, dst.ap.rearrange("(r o) -> r o", o=1)), tv(s.pT)[c.c_rw:c.c_rw + c.RWC, gl:gl + 1])


BIGSEL = 1.0e6


class KB6(KB5):
    def setup3(s):
        c = s.c
        IN, OUT = "ExternalInput", "ExternalOutput"
        L = c.DEPTH
        c.NCH = c.PAST // 512
        c.HT = c.IDXH * c.TS
        c.TC = c.TS * c.NCH
        c.GQ = c.ATH // c.KVH
        c.GT = c.GQ * c.TS
        assert c.TC <= 128 and c.PAST % 512 == 0
        s.cache_k = [s.dram(f"cache_k{i}", [c.NPOOL * 128, c.KVW], F32, IN) for i in range(L)]
        s.cache_v = [s.dram(f"cache_v{i}", [c.NPOOL * 128, c.KVW], F32, IN) for i in range(L)]
        s.cache_kidx = [s.dram(f"cache_kidx{i}", [c.NPOOL * 128, 128], F32, IN) for i in range(L)]
        s.ptab = s.dram("ptab", [1, c.NPAGES], I32, IN)
        s.k_o = s.dram("k_o", [L, c.NT, c.KVW], F32, OUT)
        s.v_o = s.dram("v_o", [L, c.NT, c.KVW], F32, OUT)
        s.ki_o = s.dram("ki_o", [L, c.NT, 128], F32, OUT)
        s.d_negdiag = s.dram("c_negdiag", [128, 128], F32, IN)
        s.d_iota = s.dram("c_iota", [128, 16], F32, IN)
        s.d_wsel = s.dram("c_wsel", [c.HT, c.TC + c.NCH], F32, IN)
        s.d_negnew = s.dram("c_negnew", [128, 16], F32, IN)
        s.d_blkc = s.dram("c_blkc", [128, 128], F32, IN)
        s.d_onesb = s.dram("c_onesb", [128, 128], BF16, IN)
        s.negdiag = s.sb_raw("negdiag", [128, 128], F32)
        s.iota = s.sb_raw("iota", [128, 16], F32)
        s.negnew = s.sb_raw("negnew", [128, 16], F32)
        s.blkc = s.sb_raw("blkc", [128, 128], F32)
        s.onesb = s.sb_raw("onesb", [128, 128], BF16)
        s.pti = s.sb_raw("pti", [128, c.NPAGES], I32)
        s.ptf = s.sb_raw("ptf", [128, c.NPAGES], F32)
        s.idxi = s.sb_raw("idxi", [128, c.NPAGES], I32)
        s.DMA(tv(s.negdiag), tv(s.d_negdiag))
        s.DMA(tv(s.iota), tv(s.d_iota))
        s.DMA(tv(s.negnew), tv(s.d_negnew))
        s.DMA(tv(s.blkc), tv(s.d_blkc))
        s.DMA(tv(s.onesb), tv(s.d_onesb))
        s.DMA(tv(s.pti), V(s.ptab, s.ptab.h[0].partition_broadcast(128)))
        s.CP(tv(s.ptf), tv(s.pti), eng="dve")
        s.TS(tv(s.ptf), tv(s.ptf), 128.0, ALU.mult, tv(s.iota)[:, 0:1], ALU.add)
        s.CP(tv(s.idxi), tv(s.ptf), eng="dve")

    def GATHER(s, out, table, idx):
        def f(e):
            return e.indirect_dma_start(out=out.ap, out_offset=None, in_=table.ap,
                                        in_offset=bass.IndirectOffsetOnAxis(ap=idx.ap, axis=0))
        s.op("pool", f, rd=[table, idx], wr=[out], dma=True)

    def bisect(s, acc, np_, S, K, sc, junk, blk=None, R=1024.0, iters=36):
        lo, mid, cnt, lB = sc[:np_, 0:1], sc[:np_, 1:2], sc[:np_, 2:3], sc[:np_, 3:4]
        s.MS(lo, -R)
        step = R
        for it in range(iters):
            s.TS(mid, lo, step, ALU.add)
            s.TS(junk[:np_, :S], acc, mid, ALU.is_ge, 0.0, ALU.add, accum=cnt)
            if blk is not None:
                ps = s.psum()
                s.MM(tv(ps)[:np_, 0:1], blk[:np_, :np_], cnt)
                cv = tv(ps)[:np_, 0:1]
            else:
                cv = cnt
            s.TS(lB, cv, K - 0.5, ALU.is_lt, -BIGSEL, ALU.mult)
            s.STT(lo, mid, lB, lo, ALU.add, ALU.max)
            step *= 0.5
        return lo

    def q_prep(s, g0, nq, qraw, qsq, rs, qTn, gq):
        c = s.c
        W = c.ATH * nq
        src = tv(s.pT)[c.c_aq:c.c_aq + c.ATW, g0:g0 + nq]
        s.DMA(qraw[:, :W].re("p (h q) -> p h q", q=nq), V(src.t, src.ap.rearrange("(h d) q -> d h q", d=128)))
        s.TT(qsq[:, :W], qraw[:, :W], qraw[:, :W], ALU.mult)
        for c0 in range(0, W, 512):
            w = min(512, W - c0)
            ps = s.psum()
            s.MM(tv(ps)[:, :w], tv(s.onesf), qsq[:, c0:c0 + w])
            s.TS(rs[:, :w], tv(ps)[:, :w], 1.0 / 128, ALU.mult, 1e-6, ALU.add)
            s.ACT(rs[:, :w], rs[:, :w], AF.Ln)
            s.ACT(rs[:, :w], rs[:, :w], AF.Exp, scale=-0.5)
            s.STT(qTn[:, c0:c0 + w], qraw[:, c0:c0 + w], gq, rs[:, :w], ALU.mult, ALU.mult)

    def phase_dsa(s, l):
        c = s.c
        s.phase("dsa")
        psf_all = s.psf
        s.psf = psf_all[:4]
        s.pfi = -1
        accO, accS = psf_all[4], psf_all[5]
        try:
            s._dsa(l, accO, accS)
        finally:
            s.psf = psf_all
            s.pfi = -1

    def _dsa(s, l, accO, accS):
        c = s.c
        NT, T, TS, KV, IH = c.NT, c.T, c.TS, c.KVH, c.IDXH
        tiles = s.tiles128([(0, NT, 0)])
        ntile = len(tiles)
        nqb = T // 128
        K_p = min(c.TOPK, T // 4)
        K_s = min(c.TOPK, (c.PAST + TS) // 4)
        scale = 128.0 ** -0.5
        r0mix = c.HGW + c.RWW
        NTp = (NT + 15) // 16 * 16
        kTn = tv(s.a16(KV * NTp, "kTn")).re("p (n t) -> p n t", t=NTp)
        kiT = tv(s.a16(NTp, "kiT"))
        Vtok = tv(s.a16(ntile * KV * 128, "Vtok")).re("p (a n d) -> p a n d", n=KV, d=128)
        gq = tv(s.a32(1, "gq"))[:, 0:1]
        gk = tv(s.a32(1, "gk"))[:, 0:1]
        wir = tv(s.a32(NTp, "wir"))
        wiT = tv(s.a32(ntile * 16, "wiT")).re("p (a h) -> p a h", h=16)
        s.col_load(gq, tv(s.q_norm)[l], 1)
        s.col_load(gk, tv(s.k_norm)[l], 1)
        raw = [tv(s.a32(512, f"raw{i}")) for i in range(2)]
        sq = tv(s.a32(512, "sq"))
        rs = tv(s.a32(512, "rs"))
        kn = tv(s.a32(512, "kn"))
        stg = [tv(s.a32(128, f"stg{i}")) for i in range(3)]
        ri = [0]
        si = [0]

        def nraw():
            ri[0] += 1
            return raw[ri[0] % 2]

        def nstg():
            si[0] += 1
            return stg[si[0] % 3]
        chunks = [(g, min(512, NT - g)) for g in range(0, NT, 512)]
        for n in range(KV):
            for (g0, nn) in chunks:
                kr = nraw()
                r0 = c.c_ak + n * 128
                s.DMA(kr[:, :nn], tv(s.pT)[r0:r0 + 128, g0:g0 + nn])
                s.TT(sq[:, :nn], kr[:, :nn], kr[:, :nn], ALU.mult)
                ps = s.psum()
                s.MM(tv(ps)[:, :nn], tv(s.onesf), sq[:, :nn])
                s.TS(rs[:, :nn], tv(ps)[:, :nn], 1.0 / 128, ALU.mult, 1e-6, ALU.add)
                s.ACT(rs[:, :nn], rs[:, :nn], AF.Ln)
                s.ACT(rs[:, :nn], rs[:, :nn], AF.Exp, scale=-0.5)
                s.STT(kn[:, :nn], kr[:, :nn], gk, rs[:, :nn], ALU.mult, ALU.mult)
                s.CP(kTn[:, n, g0:g0 + nn], kn[:, :nn], eng="act")
                for o in range(0, nn, 128):
                    m = min(128, nn - o)
                    pt = s.psum()
                    s.TR(tv(pt)[:m, 0:128], kn[:, o:o + m], tv(s.identf))
                    st_ = nstg()
                    s.CP(st_[:m, :], tv(pt)[:m, 0:128])
                    s.DMA(tv(s.k_o.sub((l, n, g0 + o)))[l, g0 + o:g0 + o + m, n * 128:(n + 1) * 128], st_[:m, :])
        for n in range(KV):
            for (g0, nn) in chunks:
                vr = nraw()
                r0 = c.c_av + n * 128
                s.DMA(vr[:, :nn], tv(s.pT)[r0:r0 + 128, g0:g0 + nn])
                for o in range(0, nn, 128):
                    m = min(128, nn - o)
                    a = (g0 + o) // 128
                    pt = s.psum()
                    s.TR(tv(pt)[:m, 0:128], vr[:, o:o + m], tv(s.identf))
                    st_ = nstg()
                    s.CP(st_[:m, :], tv(pt)[:m, 0:128], eng="act")
                    s.CP(Vtok[:m, a, n, :], tv(pt)[:m, 0:128], eng="dve")
                    s.DMA(tv(s.v_o.sub((l, n, g0 + o)))[l, g0 + o:g0 + o + m, n * 128:(n + 1) * 128], st_[:m, :])
        for (g0, nn) in chunks:
            kr = nraw()
            s.DMA(kr[:, :nn], tv(s.pT)[c.c_aki:c.c_aki + 128, g0:g0 + nn])
            s.CP(kiT[:, g0:g0 + nn], kr[:, :nn], eng="act")
            for o in range(0, nn, 128):
                m = min(128, nn - o)
                pt = s.psum()
                s.TR(tv(pt)[:m, 0:128], kr[:, o:o + m], tv(s.identf))
                st_ = nstg()
                s.CP(st_[:m, :], tv(pt)[:m, 0:128])
                s.DMA(tv(s.ki_o.sub((l, g0 + o)))[l, g0 + o:g0 + o + m, :], st_[:m, :])
        wscale = float(c.IDXH ** -0.5 * 128.0 ** -0.5)
        s.DMA(wir[:IH, :NT], tv(s.pT)[c.c_awi:c.c_awi + IH, 0:NT])
        for (col, m, g0) in tiles[:nqb]:
            a = g0 // 128
            pt = s.psum()
            s.TR(tv(pt)[:m, 0:IH], wir[:IH, g0:g0 + m], tv(s.identf)[:IH, :IH])
            s.TS(wiT[:m, a, 0:IH], tv(pt)[:m, 0:IH], wscale, ALU.mult)

        Smax = T
        acc = tv(s.a32(Smax, "acc"))
        junk = tv(s.a16(max(Smax, 528), "junk"))
        maskb = tv(s.a16(max(Smax, 528), "maskb"))
        maskT = tv(s.a16(nqb * 128, "maskT")).re("p (a q) -> p a q", q=128)
        qir = tv(s.a32(IH * 128, "qir"))
        qib = tv(s.a16(IH * 128, "qib")).re("p (h q) -> p h q", q=128)
        rl = [tv(s.a32(512, f"rl{i}")) for i in range(2)]
        qraw = tv(s.a32(c.ATH * 128, "qraw"))
        qsq = tv(s.a32(c.ATH * 128, "qsq"))
        qTn = tv(s.a16(c.ATH * 128, "qTn"))
        Eb = [tv(s.a16(512, f"E{i}")) for i in range(2)]
        PTb = [tv(s.a16(512, f"PT{i}")) for i in range(2)]
        rsum = tv(s.a32(512, "rsum"))
        o16 = [tv(s.a16(512, f"o16{i}")) for i in range(2)]
        sc = tv(s.a32(16, "bsc"))
        GQ = c.GQ
        ei = 0
        for qb in range(nqb):
            g0 = qb * 128
            S = g0 + 128
            src = tv(s.pT)[c.c_aqi:c.c_aqi + IH * 128, g0:g0 + 128]
            s.DMA(qir.re("p (h q) -> p h q", q=128), V(src.t, src.ap.rearrange("(h d) q -> d h q", d=128)))
            s.CP(qib, qir.re("p (h q) -> p h q", q=128), eng="act")
            for k0 in range(0, S, 512):
                w = min(512, S - k0)
                for h in range(IH):
                    ps = s.psum()
                    s.MM(tv(ps)[:, :w], qib[:, h, :], kiT[:, k0:k0 + w])
                    if h == 0:
                        s.TS(acc[:, k0:k0 + w], tv(ps)[:, :w], 0.0, ALU.max, wiT[:, qb, 0:1], ALU.mult)
                    else:
                        r = rl[h % 2]
                        s.ACT(r[:, :w], tv(ps)[:, :w], AF.Relu)
                        s.STT(acc[:, k0:k0 + w], r[:, :w], wiT[:, qb, h:h + 1], acc[:, k0:k0 + w], ALU.mult, ALU.add)
            s.TT(acc[:, S - 128:S], acc[:, S - 128:S], tv(s.negdiag), ALU.add)
            lo = s.bisect(acc[:, :S], 128, S, K_p, sc, junk)
            s.TS(maskb[:, :S], acc[:, :S], lo, ALU.is_ge)
            if getattr(s, "dbg_dsa", False):
                d1 = s.dram(f"dbg_acc{l}_{qb}", [128, S], F32, "ExternalOutput")
                d2 = s.dram(f"dbg_sc{l}_{qb}", [128, 4], F32, "ExternalOutput")
                s.DMA(tv(d1), acc[:, :S])
                s.DMA(tv(d2), sc[:, 0:4])
            for kt0 in range(0, qb + 1, 8):
                kn_ = min(8, qb + 1 - kt0)
                pb = s.psum(bf=True)
                for j in range(kn_):
                    s.TR(tv(pb)[:, j * 128:(j + 1) * 128], maskb[:, (kt0 + j) * 128:(kt0 + j + 1) * 128], tv(s.identb))
                s.CP(maskT[:, kt0:kt0 + kn_, :], tv(pb)[:, 0:kn_ * 128].re("p (a q) -> p a q", q=128))
            s.q_prep(g0, 128, qraw, qsq, rs, qTn, gq)
            qv = qTn.re("p (h q) -> p h q", q=128)
            for n in range(KV):
                for kt in range(qb + 1):
                    ei += 1
                    E, PT = Eb[ei % 2], PTb[ei % 2]
                    pS = s.psum()
                    s.MM(tv(pS)[:, :GQ * 128], kTn[:, n, kt * 128:(kt + 1) * 128], qTn[:, n * GQ * 128:(n + 1) * GQ * 128])
                    s.ACT(E[:, :GQ * 128], tv(pS)[:, :GQ * 128], AF.Exp, scale=scale)
                    mv = maskT[:, kt, :]
                    s.TT(PT[:, :GQ * 128].re("p (g q) -> p g q", q=128), E[:, :GQ * 128].re("p (g q) -> p g q", q=128),
                         V(mv.t, mv.ap.unsqueeze(1).to_broadcast([128, GQ, 128])), ALU.mult)
                    s.MM(tv(accO)[:, :GQ * 128], Vtok[:, kt, n, :], PT[:, :GQ * 128], start=(kt == 0), stop=(kt == qb))
                    s.MM(tv(accS)[:, :GQ * 128], tv(s.onesb), PT[:, :GQ * 128], start=(kt == 0), stop=(kt == qb))
                s.RCP(rsum[:, :GQ * 128], tv(accS)[:, :GQ * 128])
                ob = o16[n % 2]
                s.TT(ob[:, :GQ * 128], tv(accO)[:, :GQ * 128], rsum[:, :GQ * 128], ALU.mult)
                r0 = r0mix + n * GQ * 128
                dst = tv(s.mixT.sub(("at", n, g0)))[r0:r0 + GQ * 128, g0:g0 + 128]
                s.DMA(V(dst.t, dst.ap.rearrange("(g d) q -> d g q", d=128)), ob[:, :GQ * 128].re("p (g q) -> p g q", q=128))

        HT, TC, NCH, GT = c.HT, c.TC, c.NCH, c.GT
        PAST, NPG = c.PAST, c.NPAGES
        kiTp = tv(s.a16(PAST, "kiTp"))
        kipg = [tv(s.a32(128, f"kipg{i}")) for i in range(4)]
        for pg0 in range(0, NPG, 4):
            pt = s.psum()
            for j in range(4):
                pg = pg0 + j
                kp = kipg[pg % 4]
                s.GATHER(kp, tv(s.cache_kidx[l]), tv(s.idxi)[:, pg:pg + 1])
                s.TR(tv(pt)[:, j * 128:(j + 1) * 128], kp, tv(s.identf))
            s.CP(kiTp[:, pg0 * 128:(pg0 + 4) * 128], tv(pt)[:, 0:512])
        qisr = tv(s.a32(HT, "qisr"))
        qis = tv(s.a16(HT, "qis"))
        src = tv(s.pT)[c.c_aqi:c.c_aqi + IH * 128, T:T + TS]
        s.DMAs(qisr[:, :HT].re("p (h t) -> p h t", t=TS), V(src.t, src.ap.rearrange("(h d) t -> d h t", d=128)))
        s.CP(qis[:, :HT], qisr[:, :HT], eng="dve")
        wcol = tv(s.a32(1, "wcol"))
        for h in range(IH):
            srcw = tv(s.pT)[c.c_awi + h, T:T + TS]
            s.DMAs(wcol[h * TS:(h + 1) * TS, 0:1], V(srcw.t, srcw.ap.rearrange("(t o) -> t o", o=1)))
        wsel = tv(s.a32(TC + NCH, "wsel"))
        s.DMA(wsel[:HT, :TC + NCH], tv(s.d_wsel))
        s.TS(wsel[:HT, :TC + NCH], wsel[:HT, :TC + NCH], wcol[:HT, 0:1], ALU.mult)

        wtmp = [tv(s.a32(TC, f"wtmp{i}")) for i in range(2)]

        def wsv_(cch):
            w_ = wtmp[cch % 2]
            s.CP(w_[:HT, :TC], wsel[:HT, NCH - 1 - cch:NCH - 1 - cch + TC], eng="dve")
            return w_[:HT, :TC]
        Rr = [tv(s.a32(512, f"Rr{i}")) for i in range(2)]
        for cch in range(NCH):
            ps = s.psum()
            s.MM(tv(ps)[:HT, :512], qis[:, :HT], kiTp[:, cch * 512:(cch + 1) * 512])
            R_ = Rr[cch % 2]
            s.ACT(R_[:HT, :512], tv(ps)[:HT, :512], AF.Relu)
            s.MM(tv(accO)[:TC, :512], wsv_(cch), R_[:HT, :512], start=(cch == 0), stop=(cch == NCH - 1))
        ps = s.psum()
        s.MM(tv(ps)[:HT, :TS], qis[:, :HT], kiT[:, T:T + TS])
        Rn = tv(s.a32(16, "Rn"))
        s.ACT(Rn[:HT, :TS], tv(ps)[:HT, :TS], AF.Relu)
        s.MM(tv(accS)[:TC, :TS], wsv_(0), Rn[:HT, :TS])
        acc_s = tv(s.a32(528, "acc_s"))
        SS = 512 + TS
        s.CP(acc_s[:TC, 0:512], tv(accO)[:TC, 0:512], eng="dve")
        s.TT(acc_s[:TC, 512:SS], tv(accS)[:TC, 0:TS], tv(s.negnew)[:TC, 0:TS], ALU.add)
        lo = s.bisect(acc_s[:TC, :SS], TC, SS, K_s, sc, junk, blk=tv(s.blkc))
        s.TS(maskb[:TC, :SS], acc_s[:TC, :SS], lo, ALU.is_ge)
        maskTs = tv(s.a16(4 * 128, "maskTs")).re("p (j m) -> p j m", m=128)
        maskTn = tv(s.a16(128, "maskTn"))
        pb = s.psum(bf=True)
        for j in range(4):
            s.TR(tv(pb)[:, j * 128:j * 128 + TC], maskb[:TC, j * 128:(j + 1) * 128], tv(s.identb)[:TC, :TC])
        s.CP(maskTs[:, :, :TC], tv(pb)[:, 0:512].re("p (j m) -> p j m", m=128)[:, :, :TC], eng="dve")
        pb = s.psum(bf=True)
        s.TR(tv(pb)[:TS, 0:TC], maskb[:TC, 512:SS], tv(s.identb)[:TC, :TC])
        s.CP(maskTn[:TS, :TC], tv(pb)[:TS, 0:TC], eng="dve")
        qTs = tv(s.a16(c.ATH * TS, "qTs"))
        s.q_prep(T, TS, qraw, qsq, rs, qTs, gq)
        NG = KV * GT
        kpg = [tv(s.a32(c.KVW, f"kpg{i}")) for i in range(3)]
        vpg = [tv(s.a32(c.KVW, f"vpg{i}")) for i in range(3)]
        kTpg = [tv(s.a16(KV * 128, f"kTpg{i}")) for i in range(3)]
        Vp = [tv(s.a16(c.KVW, f"Vp{i}")) for i in range(3)]
        Es = [tv(s.a16(max(NG, 16), f"Es{i}")) for i in range(2)]
        PTs = [tv(s.a16(max(NG, 16), f"PTs{i}")) for i in range(2)]
        accOs = tv(s.a32(2 * NG, "accOs"))

        def attend(np_, kT_of, v_of, mview, first, i):
            pS = s.psum()
            for n in range(KV):
                s.MM(tv(pS)[:np_, n * GT:(n + 1) * GT], kT_of(n), qTs[:, n * GT:(n + 1) * GT])
            E, PT = Es[i % 2], PTs[i % 2]
            s.ACT(E[:np_, :NG], tv(pS)[:np_, :NG], AF.Exp, scale=scale)
            s.TT(PT[:np_, :NG].re("p (a t) -> p a t", t=TS), E[:np_, :NG].re("p (a t) -> p a t", t=TS),
                 V(mview.t, mview.ap.unsqueeze(1).to_broadcast([np_, KV * GQ, TS])), ALU.mult)
            pO = s.psum()
            for n in range(KV):
                s.MM(tv(pO)[:, n * GT:(n + 1) * GT], v_of(n), PT[:np_, n * GT:(n + 1) * GT])
            s.MM(tv(pO)[:, NG:2 * NG], tv(s.onesb)[:np_, :], PT[:np_, :NG])
            if first:
                s.CP(accOs[:, :2 * NG], tv(pO)[:, :2 * NG], eng="dve")
            else:
                s.TT(accOs[:, :2 * NG], accOs[:, :2 * NG], tv(pO)[:, :2 * NG], ALU.add)

        for pg in range(NPG):
            cch, j = pg // 4, pg % 4
            kp, vp = kpg[pg % 3], vpg[pg % 3]
            s.GATHER(kp, tv(s.cache_k[l]), tv(s.idxi)[:, pg:pg + 1])
            s.GATHER(vp, tv(s.cache_v[l]), tv(s.idxi)[:, pg:pg + 1])
            pt = s.psum()
            for n in range(KV):
                s.TR(tv(pt)[:, n * 128:(n + 1) * 128], kp[:, n * 128:(n + 1) * 128], tv(s.identf))
            kT_ = kTpg[pg % 3]
            s.CP(kT_[:, :KV * 128], tv(pt)[:, :KV * 128], eng="act")
            V_ = Vp[pg % 3]
            s.CP(V_[:, :c.KVW], vp[:, :c.KVW], eng="dve")
            mv = maskTs[:, j, :TC].re("p (t c) -> p t c", c=NCH)[:, :, cch]
            attend(128, lambda n: kT_[:, n * 128:(n + 1) * 128], lambda n: V_[:, n * 128:(n + 1) * 128], mv, pg == 0, pg)
        aS = (T // 128)
        mvn = maskTn[:TS, :TC].re("p (t c) -> p t c", c=NCH)[:, :, 0]
        attend(TS, lambda n: kTn[:, n, T:T + TS], lambda n: Vtok[:TS, aS, n, :], mvn, False, NPG)
        rss = tv(s.a32(NG, "rss"))
        os16 = tv(s.a16(max(NG, 16), "os16"))
        s.RCP(rss[:, :NG], accOs[:, NG:2 * NG])
        s.TT(os16[:, :NG], accOs[:, :NG], rss[:, :NG], ALU.mult)
        dst = tv(s.mixT.sub(("at", "s")))[r0mix:r0mix + c.ATW, T:T + TS]
        s.DMAs(V(dst.t, dst.ap.rearrange("(h d) t -> d h t", d=128)), os16[:, :NG].re("p (h t) -> p h t", t=TS))


def consts2(cfg):
    c = cfg
    NCH = c.PAST // 512
    HT = c.IDXH * c.TS
    TC = c.TS * NCH
    p = np.arange(128)[:, None]
    j = np.arange(128)[None, :]
    negdiag = np.where(j <= p, 0.0, -1e30).astype(np.float32)
    iota = np.tile(np.arange(128, dtype=np.float32)[:, None], (1, 16))
    wscale = float(c.IDXH ** -0.5 * 128.0 ** -0.5)
    wsel = np.zeros((HT, TC + NCH), np.float32)
    for h in range(c.IDXH):
        for t in range(c.TS):
            wsel[h * c.TS + t, t * NCH + NCH - 1] = wscale
    negnew = np.full((128, 16), -1e30, np.float32)
    for t in range(c.TS):
        for sidx in range(c.TS):
            if sidx <= t:
                negnew[t * NCH + 0, sidx] = 0.0
    blkc = np.zeros((128, 128), np.float32)
    for t in range(c.TS):
        blkc[t * NCH:(t + 1) * NCH, t * NCH:(t + 1) * NCH] = 1.0
    return dict(c_negdiag=negdiag, c_iota=iota, c_wsel=wsel, c_negnew=negnew, c_blkc=blkc,
                c_onesb=np.ones((128, 128), np.float32).astype(ml_dtypes.bfloat16))


def build_full(cfg, layers=None):
    b = KB6(cfg)
    b.setup()
    b.setup2()
    b.setup3()
    b.DMA(tv(b.xres), tv(b.x_in))
    for l in range(cfg.DEPTH if layers is None else layers):
        b.phase_win(l)
        b.phase_hgrn(l)
        b.phase_rwkv(l)
        b.phase_dsa(l)
        b.phase_wout(l)
        b.phase_ffn(l)
    b.barrier()
    b.P.emit()
    return b


def run_full(cfg, inp, layers=None):
    b = build_full(cfg, layers)
    maps = make_in_maps(cfg, inp)
    c2 = consts2(cfg)
    for m in maps:
        m.update(c2)
    maps = [{k: np.ascontiguousarray(v).astype(np_dt(b.din[k][1])) if v.dtype != np_dt(b.din[k][1]) else np.ascontiguousarray(v)
             for k, v in m.items() if k in b.din} for m in maps]
    for k in b.din:
        assert k in maps[0], k
    res = run_bass_kernel_spmd(b.nc, maps, core_ids=list(range(8)))
    return b, res.results


def assemble(cfg, R, B=4, DB=8):
    c = cfg
    L, T, TS = c.DEPTH, c.T, c.TS
    f = np.float32
    y_p = np.stack([R[b]["xres"][:T] for b in range(B)]).astype(f)
    y_s = np.stack([R[i]["xres"][T:] for i in range(DB)]).astype(f)

    def pl(name, shp):
        return np.stack([np.stack([R[b][name][l, :T].reshape(T, *shp) for b in range(B)]) for l in range(L)]).astype(f)

    def sl(name, shp):
        return np.stack([np.stack([R[i][name][l, T:].reshape(TS, *shp) for i in range(DB)]) for l in range(L)]).astype(f)

    def st(name, seq, n):
        return np.stack([np.stack([R[i][name][l, seq] for i in range(n)]) for l in range(L)]).astype(f)
    return (y_p, y_s,
            pl("k_o", (c.KVH, 128)), pl("v_o", (c.KVH, 128)), pl("ki_o", (128,)),
            st("hg_o", 0, B), st("rw_o", 0, B), st("sh_o", 0, B),
            sl("k_o", (c.KVH, 128)), sl("v_o", (c.KVH, 128)), sl("ki_o", (128,)),
            st("hg_o", 1, DB), st("rw_o", 1, DB), st("sh_o", 1, DB))


def kernel(**inputs):
    cfg = Cfg()
    inp = {k: np.asarray(v) for k, v in inputs.items()}
    b, R = run_full(cfg, inp)
    return assemble(cfg, R)
```
